# Optimizing a Trainium2 kernel written in Bass

```python
import jax, jax.numpy as jnp
from jax import lax
import numpy as np

D_MODEL = 2048
BATCH = 1
SEQ = 16384
DEPTH = 1

M_HEADS = 4
M_HEAD_DIM = 256
M_WIDTH = M_HEADS * M_HEAD_DIM
M_CHUNK = 128
CONV_WIDTH = 4
A_HEADS = 8
A_HEAD_DIM = 128
A_WIDTH = A_HEADS * A_HEAD_DIM
MOBA_BLOCK = 256
MOBA_TOPK = 3
Q_BLOCK = 128
ROPE_THETA = 10000.0
D_FF = 4 * D_MODEL
NORM_EPS = 1e-6

SPLITS = (M_WIDTH, M_WIDTH, M_WIDTH, M_WIDTH, M_HEADS, M_HEADS,
          A_WIDTH, A_WIDTH, A_WIDTH, D_MODEL, D_MODEL)
IN_COLS = 4 * M_WIDTH + 2 * M_HEADS + 3 * A_WIDTH + 2 * D_MODEL

kernel_name = 'mlstm_moba_gated_hybrid_block'


def rmsnorm(x, w):
    xf = x.astype(jnp.float32)
    y = xf * lax.rsqrt(jnp.mean(xf * xf, axis=-1, keepdims=True) + NORM_EPS)
    return (y * w.astype(jnp.float32)).astype(x.dtype)


def split_cols(u):
    outs, start = [], 0
    for width in SPLITS:
        outs.append(u[..., start:start + width])
        start += width
    return outs


def to_heads(u, n_heads):
    B, S, C = u.shape
    return u.reshape(B, S, n_heads, C // n_heads).transpose(0, 2, 1, 3)


def causal_depthwise_conv(u, w, b):
    S = u.shape[1]
    up = jnp.pad(u, ((0, 0), (CONV_WIDTH - 1, 0), (0, 0)))
    out = b
    for j in range(CONV_WIDTH):
        out = out + w[j] * up[:, j:j + S]
    return out


def apply_rope(x, positions):
    D = x.shape[-1]
    inv_freq = 1.0 / (ROPE_THETA ** (jnp.arange(0, D, 2, dtype=jnp.float32) / D))
    ang = positions.astype(jnp.float32)[..., None] * inv_freq
    ang = jnp.concatenate([ang, ang], axis=-1)[:, None]
    xf = x.astype(jnp.float32)
    x1, x2 = xf[..., :D // 2], xf[..., D // 2:]
    rot = jnp.concatenate([-x2, x1], axis=-1)
    return (xf * jnp.cos(ang) + rot * jnp.sin(ang)).astype(x.dtype)


def mlstm_chunkwise(q, k, v, i_pre, f_pre):
    B, H, S, D = q.shape
    L = M_CHUNK
    NC = S // L
    f32 = jnp.float32
    q = q.astype(f32).reshape(B, H, NC, L, D)
    k = (k.astype(f32) * (D ** -0.5)).reshape(B, H, NC, L, D)
    v = v.astype(f32).reshape(B, H, NC, L, D)
    log_f = jax.nn.log_sigmoid(f_pre.astype(f32)).reshape(B, H, NC, L)
    log_i = i_pre.astype(f32).reshape(B, H, NC, L)
    b = jnp.cumsum(log_f, axis=-1)
    F = b[..., -1]

    w = F[..., None] - b + log_i
    m_loc = jnp.max(w, axis=-1)
    e = jnp.exp(w - m_loc[..., None])
    C_loc = jnp.einsum('bhclv,bhclk->bhcvk', v * e[..., None], k)
    n_loc = jnp.einsum('bhcl,bhclk->bhck', e, k)

    def step(carry, xs):
        C, n, m = carry
        Fc, Cl, nl, ml = xs
        m_new = jnp.maximum(Fc + m, ml)
        a = jnp.exp(Fc + m - m_new)
        g = jnp.exp(ml - m_new)
        C_new = a[..., None, None] * C + g[..., None, None] * Cl
        n_new = a[..., None] * n + g[..., None] * nl
        return (C_new, n_new, m_new), (C, n, m)

    init = (jnp.zeros((B, H, D, D), f32), jnp.zeros((B, H, D), f32), jnp.zeros((B, H), f32))
    xs = (jnp.moveaxis(F, 2, 0), jnp.moveaxis(C_loc, 2, 0),
          jnp.moveaxis(n_loc, 2, 0), jnp.moveaxis(m_loc, 2, 0))
    _, (C_prev, n_prev, m_prev) = lax.scan(step, init, xs)
    C_prev = jnp.moveaxis(C_prev, 0, 2)
    n_prev = jnp.moveaxis(n_prev, 0, 2)
    m_prev = jnp.moveaxis(m_prev, 0, 2)

    a_log = b + m_prev[..., None]
    d_log = b[..., :, None] - b[..., None, :] + log_i[..., None, :]
    causal = jnp.tril(jnp.ones((L, L), dtype=bool))
    d_log = jnp.where(causal, d_log, -jnp.inf)
    m_t = jnp.maximum(a_log, jnp.max(d_log, axis=-1))
    s_ts = jnp.einsum('bhctd,bhcsd->bhcts', q, k) * jnp.exp(d_log - m_t[..., None])
    inter = jnp.exp(a_log - m_t)
    num = (inter[..., None] * jnp.einsum('bhcvk,bhctk->bhctv', C_prev, q)
           + jnp.einsum('bhcts,bhcsv->bhctv', s_ts, v))
    den = inter * jnp.einsum('bhck,bhctk->bhct', n_prev, q) + jnp.sum(s_ts, axis=-1)
    h = num / jnp.maximum(jnp.abs(den), jnp.exp(-m_t))[..., None]
    return h.reshape(B, H, S, D)


def moba_attention(q, k, v):
    B, H, S, D = q.shape
    BS = MOBA_BLOCK
    NB = -(-S // BS)
    K_SEL = min(MOBA_TOPK, NB)
    pad = NB * BS - S
    kp = jnp.pad(k, ((0, 0), (0, 0), (0, pad), (0, 0))).reshape(B, H, NB, BS, D)
    vp = jnp.pad(v, ((0, 0), (0, 0), (0, pad), (0, 0))).reshape(B, H, NB, BS, D)
    k_mean = jnp.mean(kp.astype(jnp.float32), axis=3)
    scale = D ** -0.5
    NQB = S // Q_BLOCK
    blk_ids = jnp.arange(NB)
    b_idx = jnp.arange(B)[:, None, None, None]
    h_idx = jnp.arange(H)[None, :, None, None]
    n_sel = K_SEL * BS

    def one_block(qb):
        start = qb * Q_BLOCK
        qf = lax.dynamic_slice_in_dim(q, start, Q_BLOCK, axis=2).astype(jnp.float32)
        own = start // BS
        gate = jnp.einsum('bhqd,bhnd->bhqn', qf, k_mean)
        gate = jnp.where(blk_ids < own, gate, -jnp.inf)
        _, sel = lax.top_k(gate, K_SEL)
        valid = jnp.arange(K_SEL) < own
        k_sel = kp[b_idx, h_idx, sel].astype(jnp.float32)
        v_sel = vp[b_idx, h_idx, sel].astype(jnp.float32)
        s_sel = jnp.einsum('bhqd,bhqnkd->bhqnk', qf, k_sel) * scale
        s_sel = jnp.where(valid[:, None], s_sel, -jnp.inf).reshape(B, H, Q_BLOCK, n_sel)
        k_own = lax.dynamic_index_in_dim(kp, own, axis=2, keepdims=False).astype(jnp.float32)
        v_own = lax.dynamic_index_in_dim(vp, own, axis=2, keepdims=False).astype(jnp.float32)
        s_own = jnp.einsum('bhqd,bhkd->bhqk', qf, k_own) * scale
        q_pos = start + jnp.arange(Q_BLOCK)
        k_pos = own * BS + jnp.arange(BS)
        s_own = jnp.where(k_pos[None, :] <= q_pos[:, None], s_own, -jnp.inf)
        p = jax.nn.softmax(jnp.concatenate([s_sel, s_own], axis=-1), axis=-1)
        o = (jnp.einsum('bhqm,bhqmd->bhqd', p[..., :n_sel],
                        v_sel.reshape(B, H, Q_BLOCK, n_sel, D))
             + jnp.einsum('bhqk,bhkd->bhqd', p[..., n_sel:], v_own))
        return o.astype(q.dtype)

    out = lax.map(one_block, jnp.arange(NQB))
    return jnp.moveaxis(out, 0, 2).reshape(B, H, S, D)


def setup_inputs(seed: int = 0) -> dict:
    key = jax.random.key(seed)
    ks = jax.random.split(key, 16)
    f32 = jnp.float32

    def gain(k, n):
        return 1.0 + 0.05 * jax.random.normal(k, (DEPTH, n), f32)

    def dense(k, fan_in, fan_out):
        return jax.random.normal(k, (DEPTH, fan_in, fan_out), f32) * (fan_in ** -0.5)

    x = jax.random.normal(ks[0], (BATCH, SEQ, D_MODEL), f32)
    positions = jnp.broadcast_to(jnp.arange(SEQ, dtype=jnp.int32), (BATCH, SEQ))
    f_bias = (jnp.linspace(3.0, 6.0, M_HEADS, dtype=f32)[None, :]
              + 0.1 * jax.random.normal(ks[6], (DEPTH, M_HEADS), f32))
    return {
        'x': x,
        'positions': positions,
        'norm_mix_pre': gain(ks[1], D_MODEL),
        'w_in': dense(ks[2], D_MODEL, IN_COLS),
        'conv_w': jax.random.normal(ks[3], (DEPTH, CONV_WIDTH, 2 * M_WIDTH), f32) * (CONV_WIDTH ** -0.5),
        'conv_b': 0.01 * jax.random.normal(ks[4], (DEPTH, 2 * M_WIDTH), f32),
        'i_bias': 0.1 * jax.random.normal(ks[5], (DEPTH, M_HEADS), f32),
        'f_bias': f_bias,
        'mlstm_norm': gain(ks[7], M_WIDTH),
        'w_branch_m': dense(ks[8], M_WIDTH, D_MODEL),
        'w_branch_a': dense(ks[9], A_WIDTH, D_MODEL),
        'w_out': dense(ks[10], D_MODEL, D_MODEL),
        'norm_mix_post': gain(ks[11], D_MODEL),
        'norm_ffn_pre': gain(ks[12], D_MODEL),
        'w_up': dense(ks[13], D_MODEL, D_FF),
        'w_down': dense(ks[14], D_FF, D_MODEL),
        'norm_ffn_post': gain(ks[15], D_MODEL),
    }


def reference(x, positions, norm_mix_pre, w_in, conv_w, conv_b, i_bias, f_bias, mlstm_norm,
              w_branch_m, w_branch_a, w_out, norm_mix_post, norm_ffn_pre, w_up, w_down,
              norm_ffn_post):
    dt = x.dtype
    B, S, _ = x.shape
    for l in range(DEPTH):
        xn = rmsnorm(x, norm_mix_pre[l])
        proj = jnp.einsum('bsd,dc->bsc', xn, w_in[l])
        mq, mk, mv, mo, mi, mf, aq, ak, av, g_m, g_a = split_cols(proj)

        qk = jax.nn.silu(causal_depthwise_conv(jnp.concatenate([mq, mk], axis=-1),
                                               conv_w[l], conv_b[l]))
        mq_h = to_heads(qk[..., :M_WIDTH], M_HEADS)
        mk_h = to_heads(qk[..., M_WIDTH:], M_HEADS)
        mv_h = to_heads(mv, M_HEADS)
        i_pre = jnp.transpose(mi + i_bias[l], (0, 2, 1))
        f_pre = jnp.transpose(mf + f_bias[l], (0, 2, 1))
        h = mlstm_chunkwise(mq_h, mk_h, mv_h, i_pre, f_pre)
        h = h * lax.rsqrt(jnp.mean(h * h, axis=-1, keepdims=True) + NORM_EPS)
        h = h.transpose(0, 2, 1, 3).reshape(B, S, M_WIDTH) * mlstm_norm[l].astype(jnp.float32)
        h_m = (h * jax.nn.sigmoid(mo.astype(jnp.float32))).astype(dt)

        aq_h = apply_rope(to_heads(aq, A_HEADS), positions)
        ak_h = apply_rope(to_heads(ak, A_HEADS), positions)
        av_h = to_heads(av, A_HEADS)
        o_a = moba_attention(aq_h, ak_h, av_h)
        h_a = o_a.transpose(0, 2, 1, 3).reshape(B, S, A_WIDTH)

        y_m = jnp.einsum('bsc,cd->bsd', h_m, w_branch_m[l])
        y_a = jnp.einsum('bsc,cd->bsd', h_a, w_branch_a[l])
        merged = jax.nn.sigmoid(g_m) * y_m + jax.nn.sigmoid(g_a) * y_a
        mix_out = jnp.einsum('bsd,de->bse', merged, w_out[l])
        x = x + rmsnorm(mix_out, norm_mix_post[l])

        hn = rmsnorm(x, norm_ffn_pre[l])
        u = jnp.square(jax.nn.relu(jnp.einsum('bsd,df->bsf', hn, w_up[l])))
        y = jnp.einsum('bsf,fd->bsd', u, w_down[l])
        x = x + rmsnorm(y, norm_ffn_post[l])
    return x
```

```python
import contextlib
import numpy as np
import ml_dtypes
import concourse.bass as bass
import concourse.mybir as mybir
from concourse.bass_utils import run_bass_kernel_spmd

F32 = mybir.dt.float32
BF16 = mybir.dt.bfloat16
I32 = mybir.dt.int32
AF = mybir.ActivationFunctionType
ALU = mybir.AluOpType
AX = mybir.AxisListType

NCORES = 8
D = 2048
S = 16384
TPC = S // NCORES
TT = 512
KT = D // 128
DFF = 4 * D
EPS = 1e-6


class Prog:
    ENGS = ("pe", "act", "dve", "pool", "sp")

    def __init__(self, nc):
        self.nc = nc
        self.ops = []
        self.last_w = {}
        self.readers = {}
        self.dma_cnt = {}

    def op(self, eng, fn, reads=(), writes=(), dma=None):
        deps = set()
        for r in reads:
            if r in self.last_w:
                deps.add(self.last_w[r])
        for w in writes:
            if w in self.last_w:
                deps.add(self.last_w[w])
            for rd in self.readers.get(w, ()):
                deps.add(rd)
        oid = len(self.ops)
        deps.discard(oid)
        self.ops.append(dict(eng=eng, fn=fn, deps=sorted(deps), dma=dma, ticket=None, used=False))
        for d in deps:
            self.ops[d]["used"] = True
        for r in reads:
            self.readers.setdefault(r, []).append(oid)
        for w in writes:
            self.last_w[w] = oid
            self.readers[w] = []
        return oid

    def emit(self, stack, final_wait_eng="sp"):
        nc = self.nc
        eng_cnt = {e: 0 for e in self.ENGS}
        dma_keys = []
        group_ops = []
        for o in self.ops:
            if o["dma"] is not None:
                k = o["dma"]
                if k not in self.dma_cnt:
                    self.dma_cnt[k] = 0
                    dma_keys.append(k)
                self.dma_cnt[k] += 16
                o["ticket"] = (("dma", k), self.dma_cnt[k])
                if k.startswith("g:"):
                    group_ops.append(o)
            elif o["used"]:
                eng_cnt[o["eng"]] += 1
                o["ticket"] = (("eng", o["eng"]), eng_cnt[o["eng"]])
        for o in group_ops:
            o["ticket"] = (o["ticket"][0], self.dma_cnt[o["dma"]])
        sems = {}
        for e in self.ENGS:
            sems[("eng", e)] = stack.enter_context(nc.semaphore("s_" + e))
        for k in dma_keys:
            sems[("dma", k)] = stack.enter_context(nc.semaphore("d_" + str(k)))
        final = [(("dma", k), v) for k, v in self.dma_cnt.items()]
        block = stack.enter_context(nc.Block())
        ops = self.ops

        def run_engine(eng_name, eng):
            waited = {}
            for o in ops:
                if o["eng"] != eng_name:
                    continue
                need = {}
                for d in o["deps"]:
                    t = ops[d]["ticket"]
                    if t is None:
                        continue
                    if eng_name == "pe" and ops[d]["eng"] == "pe":
                        continue
                    sk, v = t
                    if v > need.get(sk, 0):
                        need[sk] = v
                for sk, v in need.items():
                    if waited.get(sk, 0) >= v:
                        continue
                    eng.wait_ge(sems[sk], v)
                    waited[sk] = v
                ins = o["fn"](eng)
                if o["ticket"] is not None:
                    sk, v = o["ticket"]
                    ins.then_inc(sems[sk], 16 if sk[0] == "dma" else 1)
            if eng_name == final_wait_eng:
                for sk, v in final:
                    eng.wait_ge(sems[sk], v)

        @block.sync
        def _(e):
            run_engine("sp", e)

        @block.tensor
        def _(e):
            run_engine("pe", e)

        @block.scalar
        def _(e):
            run_engine("act", e)

        @block.vector
        def _(e):
            run_engine("dve", e)

        @block.gpsimd
        def _(e):
            run_engine("pool", e)


def _rmsnorm_stats(P, src_tiles, src_keys, sq, ones, ps_stat, rstd, tag, dim):
    n = len(src_tiles)
    for k in range(n):
        sqk = sq[k % 2]
        P.op("act", lambda e, a=src_tiles[k], o=sqk: e.activation(out=o[:], in_=a, func=AF.Square),
             reads=[src_keys[k]], writes=[("sq", k % 2)])
        P.op("pe", lambda e, o=sqk, k=k: e.matmul(ps_stat[:], ones[:], o[:], start=(k == 0), stop=(k == n - 1)),
             reads=[("sq", k % 2), "ones"], writes=["ps_stat"])
    P.op("act", lambda e: e.activation(out=rstd[:], in_=ps_stat[:], func=AF.Ln, bias=EPS, scale=1.0 / dim),
         reads=["ps_stat"], writes=[tag])
    P.op("act", lambda e: e.activation(out=rstd[:], in_=rstd[:], func=AF.Exp, scale=-0.5),
         reads=[tag], writes=[tag])


def build_phase_b(debug=False, NT=TPC // TT):
    nc = bass.Bass("TRN2", target_bir_lowering=False)
    xT = nc.dram_tensor("xT", [D, TPC], F32, kind="ExternalInput").ap()
    hT = nc.dram_tensor("hT", [D, TPC], BF16, kind="ExternalInput").ap()
    gam = nc.dram_tensor("gam", [128, 4 * KT], F32, kind="ExternalInput").ap()
    wg = nc.dram_tensor("wg", [32, 128, D], F32, kind="ExternalInput").ap()
    wbm = nc.dram_tensor("wbm", [16, 128, 1024], F32, kind="ExternalInput").ap()
    wba = nc.dram_tensor("wba", [16, 128, 1024], F32, kind="ExternalInput").ap()
    wo = nc.dram_tensor("wo", [16, 128, D], F32, kind="ExternalInput").ap()
    wu = nc.dram_tensor("wu", [64, 128, D], F32, kind="ExternalInput").ap()
    wd = nc.dram_tensor("wd", [16, 128, DFF], F32, kind="ExternalInput").ap()
    outT = nc.dram_tensor("outT", [D, TPC], F32, kind="ExternalOutput").ap()
    dbg = {}
    if debug:
        for nm, dt in [("xn", BF16), ("mg", BF16), ("x1", F32), ("hn", BF16), ("y", F32), ("rstd", F32), ("mixo", F32)]:
            dbg[nm] = nc.dram_tensor("dbg_" + nm, [128, KT, TT], dt, kind="ExternalOutput").ap()

    with contextlib.ExitStack() as st:
        def sb(name, shape, dt):
            return st.enter_context(nc.sbuf_tensor(name, shape, dt))

        def dump(P, nm, buf, keys):
            if debug:
                P.op("sp", lambda e: e.dma_start(out=dbg[nm], in_=buf), reads=keys, dma="dbg_" + nm)
        x0 = sb("x0", [128, KT, TT], F32)
        xn = sb("xn", [128, KT, TT], BF16)
        hb = sb("hb", [128, 2 * KT, TT], BF16)
        mix = sb("mix", [128, KT, TT], F32)
        sq = [sb("sq0", [128, TT], BF16), sb("sq1", [128, TT], BF16)]
        rstd = sb("rstd", [128, TT], F32)
        ones = sb("ones", [128, 128], BF16)
        gm = sb("gam_sb", [128, 4 * KT], F32)
        tmpa = [sb("tmpa%d" % i, [128, TT], F32) for i in range(4)]
        tmpb = [sb("tmpb%d" % i, [128, TT], F32) for i in range(2)]
        NW = 4
        wslab = [sb("wslab%d" % i, [128, 4096], BF16) for i in range(NW)]
        ps = [st.enter_context(nc.psum_tensor("ps%d" % i, [128, TT], F32)) for i in range(8)]
        ps_stat = ps[7]

        P = Prog(nc)
        P.op("pool", lambda e: e.memset(ones[:], 1.0), writes=["ones"])
        P.op("sp", lambda e: e.dma_start(out=gm[:], in_=gam[:]), writes=["gam"], dma="c")
        wcount = [0]

        def load_w(src_ap, ncols):
            i = wcount[0] % NW
            wcount[0] += 1
            P.op("pool", lambda e, i=i: e.dma_start(out=wslab[i][:, 0:ncols], in_=src_ap),
                 writes=[("w", i)], dma="w%d" % i)
            return i

        bank_rr = [0]

        def next_bank():
            b = bank_rr[0] % 7
            bank_rr[0] += 1
            return b

        xT_v = xT.rearrange("(k p) t -> p k t", p=128)
        hT_v = hT.rearrange("(k p) t -> p k t", p=128)
        outT_v = outT.rearrange("(k p) t -> p k t", p=128)

        for tt in range(NT):
            tsl = slice(tt * TT, (tt + 1) * TT)
            for k4 in range(4):
                P.op("sp", lambda e, k4=k4, tsl=tsl: e.dma_start(out=x0[:, 4 * k4:4 * k4 + 4, :], in_=xT_v[:, 4 * k4:4 * k4 + 4, tsl]),
                     writes=[("x0", k) for k in range(4 * k4, 4 * k4 + 4)], dma="x%d" % k4)
            for k4 in range(4):
                P.op("sp", lambda e, k4=k4, tsl=tsl: e.dma_start(out=hb[:, 4 * k4:4 * k4 + 4, :], in_=hT_v[:, 4 * k4:4 * k4 + 4, tsl]),
                     writes=[("hb", k) for k in range(4 * k4, 4 * k4 + 4)], dma="h%d" % k4)
            _rmsnorm_stats(P, [x0[:, k, :] for k in range(KT)], [("x0", k) for k in range(KT)], sq, ones, ps_stat, rstd, "rstd", D)
            for k in range(KT):
                P.op("dve", lambda e, k=k: e.scalar_tensor_tensor(out=xn[:, k, :], in0=x0[:, k, :], scalar=gm[:, k:k + 1], in1=rstd[:],
                                                                   op0=ALU.mult, op1=ALU.mult),
                     reads=[("x0", k), "gam", "rstd"], writes=[("xn", k)])
            if tt == 0:
                dump(P, "xn", xn[:], [("xn", k) for k in range(KT)])
                dump(P, "rstd", rstd[:], ["rstd"]) if False else None
            for j in range(KT):
                pb = [ps[(4 * (j % 2)) + i] for i in range(4)]
                pk = [("ps", (4 * (j % 2)) + i) for i in range(4)]
                if j % 2 == 1:
                    pk[3] = "ps_stat"
                srcs = [(wg[j], D, xn, 0, ("xn",)), (wg[16 + j], D, xn, 0, ("xn",)),
                        (wbm[j], 1024, hb, 0, ("hb",)), (wba[j], 1024, hb, 8, ("hb",))]
                for gi, (wsrc, K, act_t, koff, kk) in enumerate(srcs):
                    wi = load_w(wsrc, K)
                    nk = K // 128
                    for k in range(nk):
                        P.op("pe", lambda e, wi=wi, k=k, nk=nk, gi=gi, act_t=act_t, koff=koff, pb=pb:
                             e.matmul(pb[gi][:], wslab[wi][:, k * 128:(k + 1) * 128], act_t[:, koff + k, :], start=(k == 0), stop=(k == nk - 1)),
                             reads=[("w", wi), (kk[0], koff + k)], writes=[pk[gi]])
                P.op("act", lambda e, pb=pb: e.activation(out=tmpa[0][:], in_=pb[0][:], func=AF.Sigmoid), reads=[pk[0]], writes=[("ta", 0)])
                P.op("act", lambda e, pb=pb: e.activation(out=tmpa[1][:], in_=pb[1][:], func=AF.Sigmoid), reads=[pk[1]], writes=[("ta", 1)])
                P.op("dve", lambda e, pb=pb: e.tensor_tensor(out=tmpa[2][:], in0=tmpa[0][:], in1=pb[2][:], op=ALU.mult), reads=[("ta", 0), pk[2]], writes=[("ta", 2)])
                P.op("dve", lambda e, pb=pb: e.tensor_tensor(out=tmpa[3][:], in0=tmpa[1][:], in1=pb[3][:], op=ALU.mult), reads=[("ta", 1), pk[3]], writes=[("ta", 3)])
                P.op("pool", lambda e, j=j: e.tensor_tensor(out=hb[:, KT + j, :], in0=tmpa[2][:], in1=tmpa[3][:], op=ALU.add),
                     reads=[("ta", 2), ("ta", 3)], writes=[("hb", KT + j)])
            if tt == 0:
                dump(P, "mg", hb[:, KT:2 * KT, :], [("hb", KT + k) for k in range(KT)])
            for j in range(KT):
                b = next_bank()
                wi = load_w(wo[j], D)
                for k in range(KT):
                    P.op("pe", lambda e, wi=wi, k=k, b=b: e.matmul(ps[b][:], wslab[wi][:, k * 128:(k + 1) * 128], hb[:, KT + k, :], start=(k == 0), stop=(k == KT - 1)),
                         reads=[("w", wi), ("hb", KT + k)], writes=[("ps", b)])
                P.op("act", lambda e, j=j, b=b: e.activation(out=mix[:, j, :], in_=ps[b][:], func=AF.Copy), reads=[("ps", b)], writes=[("mix", j)])
            if tt == 0:
                dump(P, "mixo", mix[:], [("mix", k) for k in range(KT)])
            _rmsnorm_stats(P, [mix[:, k, :] for k in range(KT)], [("mix", k) for k in range(KT)], sq, ones, ps_stat, rstd, "rstd", D)
            for k in range(KT):
                P.op("dve", lambda e, k=k: e.scalar_tensor_tensor(out=mix[:, k, :], in0=mix[:, k, :], scalar=gm[:, KT + k:KT + k + 1], in1=rstd[:],
                                                                   op0=ALU.mult, op1=ALU.mult),
                     reads=[("mix", k), "gam", "rstd"], writes=[("mix", k)])
                P.op("pool", lambda e, k=k: e.tensor_tensor(out=x0[:, k, :], in0=x0[:, k, :], in1=mix[:, k, :], op=ALU.add),
                     reads=[("mix", k), ("x0", k)], writes=[("x0", k)])
            if tt == 0:
                dump(P, "x1", x0[:], [("x0", k) for k in range(KT)])
            _rmsnorm_stats(P, [x0[:, k, :] for k in range(KT)], [("x0", k) for k in range(KT)], sq, ones, ps_stat, rstd, "rstd", D)
            for k in range(KT):
                P.op("dve", lambda e, k=k: e.scalar_tensor_tensor(out=xn[:, k, :], in0=x0[:, k, :], scalar=gm[:, 2 * KT + k:2 * KT + k + 1], in1=rstd[:],
                                                                   op0=ALU.mult, op1=ALU.mult),
                     reads=[("x0", k), "gam", "rstd"], writes=[("xn", k)])
            if tt == 0:
                dump(P, "hn", xn[:], [("xn", k) for k in range(KT)])
            for half in range(2):
                for jj in range(32):
                    j = half * 32 + jj
                    b = next_bank()
                    wi = load_w(wu[j], D)
                    for k in range(KT):
                        P.op("pe", lambda e, wi=wi, k=k, b=b: e.matmul(ps[b][:], wslab[wi][:, k * 128:(k + 1) * 128], xn[:, k, :], start=(k == 0), stop=(k == KT - 1)),
                             reads=[("w", wi), ("xn", k)], writes=[("ps", b)])
                    tb = tmpb[jj % 2]
                    P.op("act", lambda e, b=b, tb=tb: e.activation(out=tb[:], in_=ps[b][:], func=AF.Relu), reads=[("ps", b)], writes=[("tb", jj % 2)])
                    P.op("pool", lambda e, jj=jj, tb=tb: e.tensor_tensor(out=hb[:, jj, :], in0=tb[:], in1=tb[:], op=ALU.mult),
                         reads=[("tb", jj % 2)], writes=[("hb", jj)])
                for j in range(KT):
                    b = next_bank()
                    wi = load_w(wd[j][:, half * 4096:(half + 1) * 4096], 4096)
                    for k in range(32):
                        P.op("pe", lambda e, wi=wi, k=k, b=b: e.matmul(ps[b][:], wslab[wi][:, k * 128:(k + 1) * 128], hb[:, k, :], start=(k == 0), stop=(k == 31)),
                             reads=[("w", wi), ("hb", k)], writes=[("ps", b)])
                    if half == 0:
                        P.op("act", lambda e, j=j, b=b: e.activation(out=mix[:, j, :], in_=ps[b][:], func=AF.Copy), reads=[("ps", b)], writes=[("mix", j)])
                    else:
                        P.op("dve", lambda e, j=j, b=b: e.tensor_tensor(out=mix[:, j, :], in0=mix[:, j, :], in1=ps[b][:], op=ALU.add),
                             reads=[("ps", b), ("mix", j)], writes=[("mix", j)])
            if tt == 0:
                dump(P, "y", mix[:], [("mix", k) for k in range(KT)])
            _rmsnorm_stats(P, [mix[:, k, :] for k in range(KT)], [("mix", k) for k in range(KT)], sq, ones, ps_stat, rstd, "rstd", D)
            for k in range(KT):
                P.op("dve", lambda e, k=k: e.scalar_tensor_tensor(out=mix[:, k, :], in0=mix[:, k, :], scalar=gm[:, 3 * KT + k:3 * KT + k + 1], in1=rstd[:],
                                                                   op0=ALU.mult, op1=ALU.mult),
                     reads=[("mix", k), "gam", "rstd"], writes=[("mix", k)])
                P.op("pool", lambda e, k=k: e.tensor_tensor(out=mix[:, k, :], in0=x0[:, k, :], in1=mix[:, k, :], op=ALU.add),
                     reads=[("mix", k), ("x0", k)], writes=[("mix", k)])
            for k4 in range(4):
                P.op("sp", lambda e, k4=k4, tsl=tsl: e.dma_start(out=outT_v[:, 4 * k4:4 * k4 + 4, tsl], in_=mix[:, 4 * k4:4 * k4 + 4, :]),
                     reads=[("mix", k) for k in range(4 * k4, 4 * k4 + 4)], dma="o%d" % k4)
        P.emit(st)
    return nc


def _slab(w, kt_major=True):
    K, N = w.shape
    a = w.reshape(K // 128, 128, N // 128, 128)
    return np.ascontiguousarray(a.transpose(2, 1, 0, 3)).reshape(N // 128, 128, K)


def _gam(v):
    return np.ascontiguousarray(v.reshape(KT, 128).T)


def phase_b_inputs(x, hT_all, w_in, w_branch_m, w_branch_a, w_out, w_up, w_down, gammas):
    wg = _slab(w_in[:, 7176:11272])
    wbm = _slab(w_branch_m)
    wba = _slab(w_branch_a)
    wo = _slab(w_out)
    wu = _slab(w_up)
    wd = _slab(w_down)
    gam = np.ascontiguousarray(np.concatenate([_gam(g) for g in gammas], axis=1))
    maps = []
    for c in range(NCORES):
        sl = slice(c * TPC, (c + 1) * TPC)
        maps.append(dict(xT=np.ascontiguousarray(x[sl].T), hT=np.ascontiguousarray(hT_all[:, sl]), gam=gam,
                         wg=wg, wbm=wbm, wba=wba, wo=wo, wu=wu, wd=wd))
    return maps


def run_phase_b(x, hT_all, w_in, w_branch_m, w_branch_a, w_out, w_up, w_down, gammas):
    nc = build_phase_b()
    maps = phase_b_inputs(x, hT_all, w_in, w_branch_m, w_branch_a, w_out, w_up, w_down, gammas)
    res = run_bass_kernel_spmd(nc, maps, core_ids=list(range(NCORES)))
    out = np.concatenate([r["outT"].T for r in res.results], axis=0)
    return out


NTA = S // TT
NEG = -30000.0
TWO_PI_HI = 6.28125
TWO_PI_LO = 0.0019353071795864769
MAGIC = 12582912.0
PI_SAFE = 3.141592


def build_phase_a(ntiles=NTA, s_eff=S, max_ops=None):
    nc = bass.Bass("TRN2", target_bir_lowering=False)
    xT = nc.dram_tensor("xT", [D, s_eff], F32, kind="ExternalInput").ap()
    pos = nc.dram_tensor("pos", [1, s_eff], I32, kind="ExternalInput").ap()
    gam = nc.dram_tensor("gam", [128, KT], F32, kind="ExternalInput").ap()
    wfm = nc.dram_tensor("wfm", [128, 6, D], F32, kind="ExternalInput").ap()
    wtm = nc.dram_tensor("wtm", [128, KT, 514], F32, kind="ExternalInput").ap()
    cwd = nc.dram_tensor("cw", [128, 20], F32, kind="ExternalInput").ap()
    misc = nc.dram_tensor("misc", [128, 4], F32, kind="ExternalInput").ap()
    normbc_d = nc.dram_tensor("normbc", [128, 128], F32, kind="ExternalInput").ap()
    tri_d = nc.dram_tensor("tri", [128, 128], F32, kind="ExternalInput").ap()
    identf_d = nc.dram_tensor("identf", [128, 128], F32, kind="ExternalInput").ap()
    mm_d = nc.dram_tensor("mm", [128, 896], F32, kind="ExternalInput").ap()
    e32_d = nc.dram_tensor("e32", [16, 16, 128], F32, kind="ExternalInput").ap()
    hmT = nc.dram_tensor("hmT", [128, s_eff], BF16, kind="ExternalOutput").ap()
    haT = nc.dram_tensor("haT", [128, s_eff], BF16, kind="ExternalOutput").ap()

    with contextlib.ExitStack() as st:
        def sb(name, shape, dt):
            return st.enter_context(nc.sbuf_tensor(name, shape, dt))
        wfm_sb = sb("wfm_sb", [128, 6, D], BF16)
        wtm_sb = sb("wtm_sb", [128, KT, 514], BF16)
        KTa = sb("KTa", [128, S], BF16)
        Va = sb("Va", [128, S // 128, 128], BF16)
        kmT = sb("kmT", [128, 256], F32)
        e32 = sb("e32_sb", [16, 16, 128], BF16)
        mm = sb("mm_sb", [128, 896], BF16)
        identb = sb("identb", [128, 128], BF16)
        identf = sb("identf_sb", [128, 128], F32)
        tri = sb("tri_sb", [128, 128], F32)
        onesf = sb("onesf", [128, 128], F32)
        onesb = sb("onesb", [128, 128], BF16)
        gm = sb("gm", [128, KT], F32)
        cw = sb("cw_sb", [128, 20], F32)
        msc = sb("msc", [128, 4], F32)
        nfb = sb("nfb", [128, 1], F32)
        normbc = sb("normbc_sb", [128, 128], F32)
        Cf = sb("Cf", [128, 2, 257], F32)
        Cb = sb("Cb", [128, 2, 257], BF16)
        xs = sb("xs", [128, KT, 256], F32)
        xn = sb("xn", [128, KT, TT], BF16)
        sq = [sb("sq%d" % i, [128, 256], BF16) for i in range(2)]
        rstd = sb("rstd", [128, TT], F32)
        cin = [sb("cin%d" % j, [128, TT + 3], F32) for j in range(4)]
        cacc = [sb("cacc0", [128, TT], F32)] * 2
        csg = [sb("csg0", [128, TT], F32)] * 2
        qk = [sb("qk%d" % j, [128, TT], BF16) for j in range(4)]
        posi = sb("posi", [128, TT], I32)
        ang = sb("ang", [128, TT], F32)
        rn1 = sb("rn1", [128, TT], F32)
        rr_ = sb("rr_", [128, TT], F32)
        sin_t = sb("sin_t", [128, TT], F32)
        cos_t = sb("cos_t", [128, TT], F32)
        pqf = sb("pqf", [128, TT], F32)
        pqs = sb("pqs", [128, TT], F32)
        qf = sb("qf", [128, TT], F32)
        qb = sb("qb", [128, TT], BF16)
        vb = [sb("vb%d" % i, [128, 257], BF16) for i in range(4)]
        ve = [sb("ve%d" % i, [128, 257], BF16) for i in range(4)]
        g2 = [sb("g2_%d" % i, [128, 128], F32) for i in range(4)]
        sg = sb("sg", [128, 128], F32)
        sml = sb("sml", [128, 336], F32)
        sD = [sb("sD%d" % i, [128, 128], BF16) for i in range(2)]
        ktok = [sb("ktok%d" % i, [128, 256], BF16) for i in range(2)]
        ho = sb("ho", [128, 128], BF16)
        hm_out = sb("hm_out", [128, TT], BF16)
        ha_out = sb("ha_out", [128, TT], BF16)
        g_sb = sb("g_sb", [128, 64], F32)
        top8 = sb("top8", [128, 8], F32)
        nbias = sb("nbias", [128, 64], BF16)
        biasT = sb("biasT", [16, 4, TT], BF16)
        PT = [sb("PT%d" % i, [128, TT], BF16) for i in range(3)]
        psb = [st.enter_context(nc.psum_tensor("psA%d" % i, [128, TT], F32)) for i in range(7)]
        pT = st.enter_context(nc.psum_tensor("psT", [128, 1024], BF16))

        P = Prog(nc)

        def sc_(g, i):
            c = g * 32 + i * 8
            return sml[:, c:c + 1]

        def sg_(g, n=1):
            return sml[:, g * 32:(g + n) * 32:8]

        regc = {}

        def fill_reg(e, val):
            if val not in regc:
                regc[val] = e.to_reg(val)
            return regc[val]

        def se_(n):
            c = 224 + 8 * n
            return sml[:, c:c + 1]
        P.op("sp", lambda e: e.dma_start(out=gm[:], in_=gam[:]), writes=["gm"], dma="g:cs")
        P.op("sp", lambda e: e.dma_start(out=cw[:], in_=cwd[:]), writes=["cw"], dma="g:cs")
        P.op("sp", lambda e: e.dma_start(out=msc[:], in_=misc[:]), writes=["msc"], dma="g:cs")
        P.op("sp", lambda e: e.dma_start(out=normbc[:], in_=normbc_d[:]), writes=["normbc"], dma="g:cs")
        P.op("sp", lambda e: e.dma_start(out=tri[:], in_=tri_d[:]), writes=["tri"], dma="g:cs")
        P.op("sp", lambda e: e.dma_start(out=identf[:], in_=identf_d[:]), writes=["identf"], dma="g:cs")
        P.op("pool", lambda e: e.dma_start(out=identb[:], in_=identf_d[:]), writes=["identb"], dma="g:cp")
        P.op("pool", lambda e: e.dma_start(out=mm[:], in_=mm_d[:]), writes=["mm"], dma="g:cp")
        P.op("pool", lambda e: e.dma_start(out=e32[:], in_=e32_d[:]), writes=["e32"], dma="g:cp")
        for j in range(6):
            P.op("pool", lambda e, j=j: e.dma_start(out=wfm_sb[:, j, :], in_=wfm[:, j, :]), writes=[("wfm", j)], dma="g:cp")
        for k4 in range(4):
            P.op("pool", lambda e, k4=k4: e.dma_start(out=wtm_sb[:, 4 * k4:4 * k4 + 4, :], in_=wtm[:, 4 * k4:4 * k4 + 4, :]),
                 writes=[("wtm", k) for k in range(4 * k4, 4 * k4 + 4)], dma="g:c")
        P.op("dve", lambda e: e.memset(onesf[:], 1.0), writes=["onesf"])
        P.op("dve", lambda e: e.memset(onesb[:], 1.0), writes=["onesb"])
        P.op("dve", lambda e: e.memset(Cf[:], 0.0), writes=["Cf"])
        P.op("dve", lambda e: e.memset(Cb[:], 0.0), writes=["Cb"])
        P.op("dve", lambda e: e.memset(kmT[:], 0.0), writes=["kmT"])
        P.op("dve", lambda e: e.memset(g_sb[:], NEG), writes=["g_sb"])
        for j in range(4):
            P.op("dve", lambda e, j=j: e.memset(cin[j][:], 0.0), writes=[("cin", j)])
            P.op("dve", lambda e, j=j: e.memset(vb[j][:, 256:257], 1.0), writes=[("vb", j)])
        P.op("dve", lambda e: e.tensor_scalar(out=nfb[:], in0=msc[:, 1:2], scalar1=-1.0, scalar2=None, op0=ALU.mult), reads=["msc"], writes=["nfb"])

        xT_v = xT.rearrange("(k p) t -> p k t", p=128)
        pos_b = pos.partition_broadcast(128)
        rrb = [0]

        def nb4():
            b = rrb[0] % 4
            rrb[0] += 1
            return b

        for T in range(ntiles):
            t0 = T * TT
            P.op("sp", lambda e, t0=t0: e.dma_start(out=posi[:], in_=pos_b[:, 0, t0:t0 + TT]), writes=["posi"], dma="pos")
            for hf in range(2):
                h0 = t0 + hf * 256
                for k4 in range(4):
                    P.op("sp", lambda e, k4=k4, h0=h0: e.dma_start(out=xs[:, 4 * k4:4 * k4 + 4, :], in_=xT_v[:, 4 * k4:4 * k4 + 4, h0:h0 + 256]),
                         writes=[("xs", k) for k in range(4 * k4, 4 * k4 + 4)], dma="x%d" % k4)
                stat = psb[6]
                for k in range(KT):
                    P.op("act", lambda e, k=k: e.activation(out=sq[k % 2][:], in_=xs[:, k, :], func=AF.Square),
                         reads=[("xs", k)], writes=[("sq", k % 2)])
                    P.op("pe", lambda e, k=k, hf=hf: e.matmul(stat[:, hf * 256:(hf + 1) * 256], onesb[:], sq[k % 2][:], start=(k == 0), stop=(k == KT - 1)),
                         reads=[("sq", k % 2), "onesb"], writes=[("ps", 6)])
                hs = slice(hf * 256, (hf + 1) * 256)
                P.op("act", lambda e, hs=hs: e.activation(out=rstd[:, hs], in_=stat[:, hs], func=AF.Ln, bias=EPS, scale=1.0 / D),
                     reads=[("ps", 6)], writes=["rstd"])
                P.op("act", lambda e, hs=hs: e.activation(out=rstd[:, hs], in_=rstd[:, hs], func=AF.Exp, scale=-0.5),
                     reads=["rstd"], writes=["rstd"])
                for k in range(KT):
                    P.op("dve", lambda e, k=k, hs=hs: e.scalar_tensor_tensor(out=xn[:, k, hs], in0=xs[:, k, :], scalar=gm[:, k:k + 1], in1=rstd[:, hs],
                                                                            op0=ALU.mult, op1=ALU.mult),
                         reads=[("xs", k), "gm", "rstd"], writes=[("xn", k)])
            P.op("dve", lambda e: e.tensor_copy(out=ang[:], in_=posi[:]), reads=["posi"], writes=["ang"])
            P.op("dve", lambda e: e.tensor_scalar(out=ang[:], in0=ang[:], scalar1=msc[:, 2:3], scalar2=None, op0=ALU.mult), reads=["ang", "msc"], writes=["ang"])
            P.op("dve", lambda e: e.tensor_scalar(out=rn1[:], in0=ang[:], scalar1=float(1.0 / (2 * np.pi)), scalar2=MAGIC, op0=ALU.mult, op1=ALU.add),
                 reads=["ang"], writes=["rn1"])
            P.op("dve", lambda e: e.tensor_scalar(out=rn1[:], in0=rn1[:], scalar1=-MAGIC, scalar2=None, op0=ALU.add), reads=["rn1"], writes=["rn1"])
            P.op("dve", lambda e: e.scalar_tensor_tensor(out=rr_[:], in0=rn1[:], scalar=-TWO_PI_HI, in1=ang[:], op0=ALU.mult, op1=ALU.add),
                 reads=["rn1", "ang"], writes=["rr_"])
            P.op("dve", lambda e: e.scalar_tensor_tensor(out=rr_[:], in0=rn1[:], scalar=-TWO_PI_LO, in1=rr_[:], op0=ALU.mult, op1=ALU.add),
                 reads=["rn1", "rr_"], writes=["rr_"])
            P.op("dve", lambda e: e.tensor_scalar(out=rr_[:], in0=rr_[:], scalar1=-PI_SAFE, scalar2=PI_SAFE, op0=ALU.max, op1=ALU.min),
                 reads=["rr_"], writes=["rr_"])
            P.op("act", lambda e: e.activation(out=sin_t[:], in_=rr_[:], func=AF.Sin), reads=["rr_"], writes=["sin_t"])
            P.op("dve", lambda e: e.scalar_tensor_tensor(out=rn1[:], in0=rr_[:], scalar=-1.0, in1=rr_[:], op0=ALU.mult, op1=ALU.max), reads=["rr_"], writes=["rn1"])
            P.op("act", lambda e: e.activation(out=cos_t[:], in_=rn1[:], func=AF.Sin, scale=-1.0, bias=float(np.pi / 2)), reads=["rn1"], writes=["cos_t"])
            for j in range(6):
                b = nb4()
                for k in range(KT):
                    P.op("pe", lambda e, j=j, k=k, b=b: e.matmul(psb[b][:], wfm_sb[:, j, k * 128:(k + 1) * 128], xn[:, k, :], start=(k == 0), stop=(k == KT - 1)),
                         reads=[("wfm", j), ("xn", k)], writes=[("ps", b)])
                if j < 4:
                    P.op("pool", lambda e, j=j: e.tensor_copy(out=cin[j][:, 0:3], in_=cin[j][:, TT:TT + 3]), reads=[("cin", j)], writes=[("cin", j)])
                    P.op("act", lambda e, j=j, b=b: e.activation(out=cin[j][:, 3:TT + 3], in_=psb[b][:], func=AF.Copy), reads=[("ps", b)], writes=[("cin", j)])
                    ca = cacc[j % 2]
                    cs = csg[j % 2]
                    P.op("dve", lambda e, j=j, ca=ca: e.tensor_scalar(out=ca[:], in0=cin[j][:, 3:TT + 3], scalar1=cw[:, 4 * j + 3:4 * j + 4], scalar2=cw[:, 16 + j:17 + j],
                                                                       op0=ALU.mult, op1=ALU.add),
                         reads=[("cin", j), "cw"], writes=[("cacc", 0)])
                    for tap in range(3):
                        P.op("dve", lambda e, j=j, ca=ca, tap=tap: e.scalar_tensor_tensor(out=ca[:], in0=cin[j][:, tap:tap + TT], scalar=cw[:, 4 * j + tap:4 * j + tap + 1],
                                                                                         in1=ca[:], op0=ALU.mult, op1=ALU.add),
                             reads=[("cin", j), "cw", ("cacc", 0)], writes=[("cacc", 0)])
                    P.op("act", lambda e, ca=ca, cs=cs: e.activation(out=cs[:], in_=ca[:], func=AF.Sigmoid), reads=[("cacc", 0)], writes=[("csg", 0)])
                    sc = 1.0 if j < 2 else 0.0625
                    P.op("dve", lambda e, j=j, ca=ca, cs=cs, sc=sc: e.scalar_tensor_tensor(out=qk[j][:], in0=ca[:], scalar=sc, in1=cs[:], op0=ALU.mult, op1=ALU.mult),
                         reads=[("cacc", 0), ("csg", 0)], writes=[("qk", j)])
                else:
                    P.op("act", lambda e, b=b: e.activation(out=pqf[:], in_=psb[b][:], func=AF.Copy), reads=[("ps", b)], writes=["pqf"])
                    P.op("act", lambda e, b=b: e.activation(out=pqs[0:64, :], in_=psb[b][64:128, :], func=AF.Copy), reads=[("ps", b)], writes=["pqs0"])
                    P.op("act", lambda e, b=b: e.activation(out=pqs[64:128, :], in_=psb[b][0:64, :], func=AF.Copy), reads=[("ps", b)], writes=["pqs1"])
                    P.op("dve", lambda e: e.tensor_tensor(out=pqf[:], in0=pqf[:], in1=cos_t[:], op=ALU.mult), reads=["pqf", "cos_t"], writes=["pqf"])
                    P.op("dve", lambda e: e.scalar_tensor_tensor(out=pqs[:], in0=pqs[:], scalar=msc[:, 3:4], in1=sin_t[:], op0=ALU.mult, op1=ALU.mult),
                         reads=["pqs0", "pqs1", "sin_t", "msc"], writes=["pqs0", "pqs1"])
                    if j == 4:
                        P.op("pool", lambda e: e.tensor_tensor(out=qf[:], in0=pqf[:], in1=pqs[:], op=ALU.add), reads=["pqf", "pqs0", "pqs1"], writes=["qf"])
                        P.op("pool", lambda e: e.tensor_copy(out=qb[:], in_=qf[:]), reads=["qf"], writes=["qb"])
                    else:
                        P.op("pool", lambda e: e.tensor_tensor(out=pqf[:], in0=pqf[:], in1=pqs[:], op=ALU.add), reads=["pqf", "pqs0", "pqs1"], writes=["pqf"])
                        P.op("pool", lambda e, t0=t0: e.tensor_copy(out=KTa[:, t0:t0 + TT], in_=pqf[:]), reads=["pqf"], writes=[("KT", T)])
                        P.op("dve", lambda e, T=T: e.tensor_reduce(out=kmT[:, 8 * T:8 * T + 8:4], in_=pqf[:].rearrange("p (b k) -> p b k", b=2), axis=AX.X, op=ALU.add),
                             reads=["pqf"], writes=["kmT"])
                        P.op("dve", lambda e, T=T: e.tensor_scalar(out=kmT[:, 8 * T:8 * T + 8:4], in0=kmT[:, 8 * T:8 * T + 8:4], scalar1=1.0 / 256, scalar2=None, op0=ALU.mult),
                             reads=["kmT"], writes=["kmT"])
            for i in range(4):
                ts_ = slice(i * 128, (i + 1) * 128)
                b1 = nb4()
                for k in range(KT):
                    P.op("pe", lambda e, k=k, b1=b1, ts_=ts_: e.matmul(psb[b1][:, 0:258], xn[:, k, ts_], wtm_sb[:, k, 0:258], start=(k == 0), stop=(k == KT - 1)),
                         reads=[("xn", k), ("wtm", k)], writes=[("ps", b1)])
                b2 = nb4()
                for k in range(KT):
                    P.op("pe", lambda e, k=k, b2=b2, ts_=ts_: e.matmul(psb[b2][:, 0:256], xn[:, k, ts_], wtm_sb[:, k, 258:514], start=(k == 0), stop=(k == KT - 1)),
                         reads=[("xn", k), ("wtm", k)], writes=[("ps", b2)])
                P.op("act", lambda e, i=i, b1=b1: e.activation(out=vb[i][:, 0:256], in_=psb[b1][:, 0:256], func=AF.Copy), reads=[("ps", b1)], writes=[("vb", i)])
                P.op("act", lambda e, i=i, b1=b1: e.activation(out=sml[:, 288 + 8 * i:288 + 8 * i + 2], in_=psb[b1][:, 256:258], func=AF.Copy),
                     reads=[("ps", b1)], writes=[("raw", i)])
                P.op("dve", lambda e, i=i: e.tensor_scalar(out=sc_(1, i), in0=sml[:, 288 + 8 * i:289 + 8 * i], scalar1=msc[:, 0:1], scalar2=None, op0=ALU.add),
                     reads=[("raw", i), "msc"], writes=[("li", i)])
                P.op("act", lambda e, i=i: e.activation(out=sc_(2, i), in_=sml[:, 289 + 8 * i:290 + 8 * i], func=AF.Exp, scale=-1.0, bias=nfb[:, 0:1]),
                     reads=[("raw", i), "nfb"], writes=[("tmp", i)])
                P.op("act", lambda e, i=i: e.activation(out=sc_(0, i), in_=sc_(2, i), func=AF.Ln, bias=1.0), reads=[("tmp", i)], writes=[("nl", i)])
                P.op("act", lambda e, b2=b2: e.activation(out=sg[:], in_=psb[b2][:, 0:128], func=AF.Sigmoid), reads=[("ps", b2)], writes=["sg"])
                P.op("act", lambda e, b2=b2, T=T, i=i: e.activation(out=Va[:, 4 * T + i, :], in_=psb[b2][:, 128:256], func=AF.Copy), reads=[("ps", b2)], writes=[("V", T)])
                P.op("pool", lambda e, i=i: e.tensor_tensor(out=g2[i][:], in0=sg[:], in1=normbc[:], op=ALU.mult), reads=["sg", "normbc"], writes=[("g2", i)])
            P.op("pe", lambda e: e.matmul(psb[6][:, 0:4], tri[:], sg_(0), start=True, stop=True), reads=[("nl", i) for i in range(4)] + ["tri"], writes=[("ps", 6)])
            P.op("pe", lambda e: e.matmul(psb[6][:, 4:8], onesf[:], sg_(0), start=True, stop=True), reads=[("nl", i) for i in range(4)] + ["onesf"], writes=[("ps", 6)])
            P.op("act", lambda e: e.activation(out=sml[:, 320:328], in_=psb[6][:, 0:8], func=AF.Copy), reads=[("ps", 6)], writes=["cums"])
            P.op("dve", lambda e: e.tensor_tensor(out=sg_(2), in0=sml[:, 320:324], in1=sg_(1), op=ALU.add),
                 reads=["cums"] + [("li", i) for i in range(4)], writes=[("tmp", i) for i in range(4)])
            P.op("act", lambda e: e.activation(out=sg_(3), in_=sg_(2), func=AF.Exp), reads=[("tmp", i) for i in range(4)], writes=["ew"])
            P.op("act", lambda e: e.activation(out=sg_(4, 2), in_=sml[:, 320:328], func=AF.Exp, scale=-1.0), reads=["cums"], writes=["ainA"])
            P.op("dve", lambda e: e.tensor_tensor(out=sg_(6), in0=sg_(3), in1=sg_(5), op=ALU.mult), reads=["ew", "ainA"], writes=["eloc"])
            for i in range(4):
                cs_ = slice(i * 128, (i + 1) * 128)
                P.op("dve", lambda e, i=i: e.tensor_scalar(out=ve[i][:], in0=vb[i][:], scalar1=sc_(6, i), scalar2=None, op0=ALU.mult),
                     reads=[("vb", i), "eloc"], writes=[("ve", i)])
                bS = i % 2
                for dt in range(2):
                    P.op("pe", lambda e, dt=dt, bS=bS, cs_=cs_: e.matmul(psb[bS][:, 0:128], qk[2 + dt][:, cs_], qk[dt][:, cs_], start=(dt == 0), stop=(dt == 1)),
                         reads=[("qk", 2 + dt), ("qk", dt)], writes=[("ps", bS)])
                P.op("dve", lambda e, i=i, bS=bS: e.scalar_tensor_tensor(out=sD[i % 2][:], in0=psb[bS][:, 0:128], scalar=sc_(3, i), in1=tri[:],
                                                                        op0=ALU.mult, op1=ALU.mult),
                     reads=[("ps", bS), "ew", "tri"], writes=[("sD", i % 2)])
                for dt in range(2):
                    P.op("pe", lambda e, dt=dt, cs_=cs_: e.transpose(pT[:, dt * 128:(dt + 1) * 128], qk[2 + dt][:, cs_], identb[:]),
                         reads=[("qk", 2 + dt), "identb"], writes=["pT"])
                P.op("act", lambda e, i=i: e.activation(out=ktok[i % 2][:], in_=pT[:, 0:256], func=AF.Copy), reads=["pT", "pT"], writes=[("ktok", i % 2)])
                for kd in range(2):
                    P.op("pe", lambda e, kd=kd, i=i: e.matmul(psb[2 + kd][:, 0:257], ktok[i % 2][:, kd * 128:(kd + 1) * 128], ve[i][:], start=True, stop=True),
                         reads=[("ktok", i % 2), ("ve", i)], writes=[("ps", 2 + kd)])
                bH = 4 + (i % 2)
                P.op("pe", lambda e, bH=bH, cs_=cs_: e.matmul(psb[bH][:, 0:257], qk[0][:, cs_], Cb[:, 0, :], start=True, stop=False),
                     reads=[("qk", 0), "Cb"], writes=[("ps", bH)])
                P.op("pe", lambda e, bH=bH, cs_=cs_: e.matmul(psb[bH][:, 0:257], qk[1][:, cs_], Cb[:, 1, :], start=False, stop=False),
                     reads=[("qk", 1), "Cb"], writes=[("ps", bH)])
                P.op("pe", lambda e, bH=bH, i=i: e.matmul(psb[bH][:, 0:257], sD[i % 2][:], vb[i][:], start=False, stop=True),
                     reads=[("sD", i % 2), ("vb", i)], writes=[("ps", bH)])
                for kd in range(2):
                    P.op("dve", lambda e, kd=kd, i=i: e.scalar_tensor_tensor(out=Cf[:, kd, :], in0=Cf[:, kd, :], scalar=sc_(5, i), in1=psb[2 + kd][:, 0:257],
                                                                            op0=ALU.mult, op1=ALU.add),
                         reads=["Cf", "ainA", ("ps", 2 + kd)], writes=["Cf"])
                P.op("pool", lambda e: e.tensor_copy(out=Cb[:], in_=Cf[:]), reads=["Cf"], writes=["Cb"])
                c0 = 28
                P.op("dve", lambda e, bH=bH, i=i: e.tensor_tensor(out=se_(0), in0=psb[bH][:, 256:257], in1=sc_(4, i), op=ALU.mult),
                     reads=[("ps", bH), "ainA"], writes=["ep0"])
                P.op("dve", lambda e: e.scalar_tensor_tensor(out=se_(1), in0=se_(0), scalar=-1.0, in1=se_(0), op0=ALU.mult, op1=ALU.max), reads=["ep0"], writes=["ep1"])
                P.op("dve", lambda e: e.tensor_single_scalar(out=se_(1), in_=se_(1), scalar=1.0, op=ALU.max), reads=["ep1"], writes=["ep1"])
                P.op("dve", lambda e: e.reciprocal(out=se_(2), in_=se_(1)), reads=["ep1"], writes=["ep2"])
                P.op("dve", lambda e, i=i: e.tensor_tensor(out=se_(3), in0=se_(2), in1=sc_(4, i), op=ALU.mult),
                     reads=["ep2", "ainA"], writes=["ep3"])
                P.op("act", lambda e, bH=bH: e.activation(out=csg[0][:, 0:256], in_=psb[bH][:, 0:256], func=AF.Square, scale=se_(3), accum_out=se_(4)),
                     reads=[("ps", bH), "ep3"], writes=["ep4", ("csg", 0)])
                P.op("act", lambda e: e.activation(out=se_(5), in_=se_(4), func=AF.Ln, bias=EPS, scale=1.0 / 256), reads=["ep4"], writes=["ep5"])
                P.op("act", lambda e: e.activation(out=se_(5), in_=se_(5), func=AF.Exp, scale=-0.5), reads=["ep5"], writes=["ep5"])
                P.op("dve", lambda e: e.tensor_tensor(out=se_(6), in0=se_(3), in1=se_(5), op=ALU.mult), reads=["ep3", "ep5"], writes=["ep6"])
                P.op("dve", lambda e, bH=bH, i=i: e.scalar_tensor_tensor(out=ho[:], in0=psb[bH][:, 0:128], scalar=se_(6), in1=g2[i][:], op0=ALU.mult, op1=ALU.mult),
                     reads=[("ps", bH), "ep6", ("g2", i)], writes=["ho"])
                P.op("pe", lambda e: e.transpose(pT[:, 512:640], ho[:], identb[:]), reads=["ho", "identb"], writes=["pT"])
                P.op("act", lambda e, cs_=cs_: e.activation(out=hm_out[:, cs_], in_=pT[:, 512:640], func=AF.Copy), reads=["pT"], writes=["hm_out"])
            P.op("sp", lambda e, t0=t0: e.dma_start(out=hmT[:, t0:t0 + TT], in_=hm_out[:]), reads=["hm_out"], dma="om")
            for i in range(4):
                own = 2 * T + i // 2
                qs_ = slice(i * 128, (i + 1) * 128)
                if own > 0:
                    P.op("pe", lambda e, qs_=qs_: e.matmul(psb[6][:, 64:128], qf[:, qs_], kmT[:, 0:256:4], start=True, stop=True), reads=["qf", "kmT"], writes=[("ps", 6)])
                    P.op("act", lambda e, own=own: e.activation(out=g_sb[:, 0:own], in_=psb[6][:, 64:64 + own], func=AF.Copy), reads=[("ps", 6)], writes=["g_sb"])
                P.op("dve", lambda e: e.max(out=top8[:], in_=g_sb[:]), reads=["g_sb"], writes=["top8"])
                P.op("dve", lambda e: e.tensor_scalar(out=nbias[:], in0=g_sb[:], scalar1=top8[:, 2:3], scalar2=NEG, op0=ALU.is_lt, op1=ALU.mult),
                     reads=["g_sb", "top8"], writes=["nbias"])
                P.op("pool", lambda e, own=own: e.affine_select(out=nbias[:], in_=nbias[:], pattern=[[-1, 64]], compare_op=ALU.is_ge, fill=fill_reg(e, 0.0),
                                                                base=own - 1, channel_multiplier=0), reads=["nbias"], writes=["nbias"])
                if i < 2:
                    P.op("pool", lambda e, own=own: e.affine_select(out=nbias[:], in_=nbias[:], pattern=[[1, 64]], compare_op=ALU.not_equal, fill=fill_reg(e, NEG),
                                                                    base=-(own + 1), channel_multiplier=0), reads=["nbias"], writes=["nbias"])
                for hh in range(4):
                    P.op("pe", lambda e, hh=hh: e.transpose(pT[0:16, 256 + hh * 128:256 + (hh + 1) * 128], nbias[:, 16 * hh:16 * hh + 16], identb[:]),
                         reads=["nbias", "identb"], writes=["pT"])
                    P.op("act", lambda e, hh=hh, qs_=qs_: e.activation(out=biasT[:, hh, qs_], in_=pT[0:16, 256 + hh * 128:256 + (hh + 1) * 128], func=AF.Copy),
                         reads=["pT"], writes=["biasT"])
            nkt = 4 * T + 4
            pO, pSm = psb[4], psb[5]

            def QK(kk):
                b = kk % 3
                j = kk // 2
                cur = kk >= 4 * T
                P.op("pe", lambda e, kk=kk, b=b: e.matmul(psb[b][:], KTa[:, kk * 128:(kk + 1) * 128], qb[:], start=True, stop=False),
                     reads=[("KT", kk // 4), "qb"], writes=[("ps", b)])
                P.op("pe", lambda e, j=j, b=b, cur=cur: e.matmul(psb[b][:], e32[:, j % 16, :], biasT[:, j // 16, :], start=False, stop=(not cur)),
                     reads=["e32", "biasT"], writes=[("ps", b)])
                if cur:
                    m = kk - 4 * T
                    P.op("pe", lambda e, m=m, b=b: e.matmul(psb[b][:], identb[:], mm[:, (3 - m) * 128:(3 - m) * 128 + TT], start=False, stop=True),
                         reads=["identb", "mm"], writes=[("ps", b)])

            QK(0)
            if nkt > 1:
                QK(1)
            for kk in range(nkt):
                b = kk % 3
                P.op("act", lambda e, b=b: e.activation(out=PT[b][:], in_=psb[b][:], func=AF.Exp, scale=float(128 ** -0.5)), reads=[("ps", b)], writes=[("PT", b)])
                if kk + 2 < nkt:
                    QK(kk + 2)
                P.op("pe", lambda e, kk=kk, b=b, nkt=nkt: e.matmul(pO[:], Va[:, kk, :], PT[b][:], start=(kk == 0), stop=(kk == nkt - 1)),
                     reads=[("V", kk // 4), ("PT", b)], writes=[("ps", 4)])
                P.op("pe", lambda e, kk=kk, b=b, nkt=nkt: e.matmul(pSm[:], onesb[:], PT[b][:], start=(kk == 0), stop=(kk == nkt - 1)),
                     reads=["onesb", ("PT", b)], writes=[("ps", 5)])
            P.op("dve", lambda e: e.reciprocal(out=pqs[:], in_=pSm[:]), reads=[("ps", 5)], writes=["pqs0", "pqs1"])
            P.op("dve", lambda e: e.tensor_tensor(out=ha_out[:], in0=pO[:], in1=pqs[:], op=ALU.mult), reads=[("ps", 4), "pqs0", "pqs1"], writes=["ha_out"])
            P.op("sp", lambda e, t0=t0: e.dma_start(out=haT[:, t0:t0 + TT], in_=ha_out[:]), reads=["ha_out"], dma="oa")
        if max_ops is not None:
            P.ops = P.ops[:max_ops]
        P.emit(st)
    return nc


W_MQ, W_MK, W_MV, W_MO, W_MI, W_MF, W_AQ, W_AK, W_AV, W_GM = 0, 1024, 2048, 3072, 4096, 4100, 4104, 5128, 6152, 7176


def phase_a_inputs(x, positions, norm_mix_pre, w_in, conv_w, conv_b, i_bias, f_bias, mlstm_norm):
    xT = np.ascontiguousarray(x.T)
    gam = _gam(norm_mix_pre)
    tri = np.triu(np.ones((128, 128), np.float32))
    identf = np.eye(128, dtype=np.float32)
    mm = np.zeros((128, 896), np.float32)
    r = np.arange(128)[:, None]
    cq = np.arange(128)[None, :]
    mm[:, 256:384] = NEG
    mm[:, 384:512] = np.where(r > cq, NEG, 0.0)
    e32 = np.zeros((16, 16, 128), np.float32)
    for j in range(16):
        e32[j, j, :] = 1.0
    inv_freq = (1.0 / (10000.0 ** (np.arange(0, 128, 2, dtype=np.float32) / np.float32(128)))).astype(np.float32)
    maps = []
    for c in range(NCORES):
        h, half = c // 2, c % 2
        fm_cols = np.concatenate([np.arange(W_MQ + 256 * h, W_MQ + 256 * h + 256), np.arange(W_MK + 256 * h, W_MK + 256 * h + 256),
                                  np.arange(W_AQ + 128 * c, W_AQ + 128 * c + 128), np.arange(W_AK + 128 * c, W_AK + 128 * c + 128)])
        own = np.arange(W_MV + 256 * h + 128 * half, W_MV + 256 * h + 128 * half + 128)
        oth = np.arange(W_MV + 256 * h + 128 * (1 - half), W_MV + 256 * h + 128 * (1 - half) + 128)
        tm_cols = np.concatenate([own, oth, [W_MI + h, W_MF + h],
                                  np.arange(W_MO + 256 * h + 128 * half, W_MO + 256 * h + 128 * half + 128),
                                  np.arange(W_AV + 128 * c, W_AV + 128 * c + 128)])
        wf = w_in[:, fm_cols]
        wfm = np.ascontiguousarray(wf.reshape(KT, 128, 6, 128).transpose(1, 2, 0, 3)).reshape(128, 6, D)
        wt = w_in[:, tm_cols]
        wtm = np.ascontiguousarray(wt.reshape(KT, 128, 514).transpose(1, 0, 2))
        chs = [256 * h, 256 * h + 128, 1024 + 256 * h, 1024 + 256 * h + 128]
        cw = np.zeros((128, 20), np.float32)
        for j, ch in enumerate(chs):
            cw[:, 4 * j:4 * j + 4] = conv_w[:, ch:ch + 128].T
            cw[:, 16 + j] = conv_b[ch:ch + 128]
        misc = np.zeros((128, 4), np.float32)
        misc[:, 0] = i_bias[h]
        misc[:, 1] = f_bias[h]
        misc[:, 2] = np.concatenate([inv_freq, inv_freq])
        misc[0:64, 3] = -1.0
        misc[64:128, 3] = 1.0
        v0 = 256 * h + 128 * half
        normbc = np.ascontiguousarray(np.broadcast_to(mlstm_norm[v0:v0 + 128][None, :], (128, 128))).astype(np.float32)
        maps.append(dict(xT=xT, pos=np.ascontiguousarray(positions.reshape(1, S)).astype(np.int32), gam=gam, wfm=wfm, wtm=wtm, cw=cw, misc=misc,
                         normbc=normbc, tri=tri, identf=identf, mm=mm, e32=e32))
    return maps


def run_phase_a(x, positions, norm_mix_pre, w_in, conv_w, conv_b, i_bias, f_bias, mlstm_norm, ntiles=NTA):
    nc = build_phase_a(ntiles)
    maps = phase_a_inputs(x, positions, norm_mix_pre, w_in, conv_w, conv_b, i_bias, f_bias, mlstm_norm)
    res = run_bass_kernel_spmd(nc, maps, core_ids=list(range(NCORES)))
    hT_all = np.zeros((D, S), ml_dtypes.bfloat16)
    for c in range(NCORES):
        h, half = c // 2, c % 2
        v0 = 256 * h + 128 * half
        hT_all[v0:v0 + 128] = res.results[c]["hmT"]
        hT_all[1024 + 128 * c:1024 + 128 * c + 128] = res.results[c]["haT"]
    return hT_all


def kernel(x, positions, norm_mix_pre, w_in, conv_w, conv_b, i_bias, f_bias, mlstm_norm,
           w_branch_m, w_branch_a, w_out, norm_mix_post, norm_ffn_pre, w_up, w_down, norm_ffn_post):
    x2 = np.asarray(x)[0]
    hT_all = run_phase_a(x2, np.asarray(positions)[0], np.asarray(norm_mix_pre)[0], np.asarray(w_in)[0], np.asarray(conv_w)[0],
                         np.asarray(conv_b)[0], np.asarray(i_bias)[0], np.asarray(f_bias)[0], np.asarray(mlstm_norm)[0])
    gammas = [np.asarray(norm_mix_pre)[0], np.asarray(norm_mix_post)[0], np.asarray(norm_ffn_pre)[0], np.asarray(norm_ffn_post)[0]]
    out = run_phase_b(x2, hT_all, np.asarray(w_in)[0], np.asarray(w_branch_m)[0], np.asarray(w_branch_a)[0], np.asarray(w_out)[0],
                      np.asarray(w_up)[0], np.asarray(w_down)[0], gammas)
    return out[None].astype(np.float32)
```

```python
import contextlib
import numpy as np
import ml_dtypes
import concourse.bass as bass
import concourse.mybir as mybir
from concourse.bass_utils import run_bass_kernel_spmd

F32 = mybir.dt.float32
BF16 = mybir.dt.bfloat16
I32 = mybir.dt.int32
AF = mybir.ActivationFunctionType
ALU = mybir.AluOpType
AX = mybir.AxisListType

NCORES = 8
D = 2048
S = 16384
TPC = S // NCORES
TT = 512
KT = D // 128
DFF = 4 * D
EPS = 1e-6


class Prog:
    ENGS = ("pe", "act", "dve", "pool", "sp")

    def __init__(self, nc, tag=""):
        self.nc = nc
        self.tag = tag
        self.ops = []
        self.last_w = {}
        self.readers = {}
        self.dma_cnt = {}

    def op(self, eng, fn, reads=(), writes=(), dma=None, inc=None):
        deps = set()
        for r in reads:
            if r in self.last_w:
                deps.add(self.last_w[r])
        for w in writes:
            if w in self.last_w:
                deps.add(self.last_w[w])
            for rd in self.readers.get(w, ()):
                deps.add(rd)
        oid = len(self.ops)
        deps.discard(oid)
        self.ops.append(dict(eng=eng, fn=fn, deps=sorted(deps), dma=dma, ticket=None, used=False))
        if inc is not None:
            self.ops[-1]["inc"] = inc
        for d in deps:
            self.ops[d]["used"] = True
        for r in reads:
            self.readers.setdefault(r, []).append(oid)
        for w in writes:
            self.last_w[w] = oid
            self.readers[w] = []
        return oid

    def emit(self, stack, final_wait_eng="sp", sem_stack=None, wait_phase=None, signal_phase=None):
        nc = self.nc
        sem_stack = sem_stack if sem_stack is not None else stack
        last_of = {}
        for o in self.ops:
            last_of[o["eng"]] = o
        for o in last_of.values():
            o["used"] = True
        eng_cnt = {e: 0 for e in self.ENGS}
        dma_keys = []
        group_ops = []
        for o in self.ops:
            if o["dma"] is not None:
                k = o["dma"]
                if k not in self.dma_cnt:
                    self.dma_cnt[k] = 0
                    dma_keys.append(k)
                self.dma_cnt[k] += o.get("inc", 16)
                o["ticket"] = (("dma", k), self.dma_cnt[k])
                if k.startswith("g:"):
                    group_ops.append(o)
            elif o["used"]:
                eng_cnt[o["eng"]] += 1
                o["ticket"] = (("eng", o["eng"]), eng_cnt[o["eng"]])
        for o in group_ops:
            o["ticket"] = (o["ticket"][0], self.dma_cnt[o["dma"]])
        sems = {}
        for e in self.ENGS:
            sems[("eng", e)] = sem_stack.enter_context(nc.semaphore(self.tag + "s_" + e))
        for k in dma_keys:
            sems[("dma", k)] = sem_stack.enter_context(nc.semaphore(self.tag + "d_" + str(k).replace(":", "_")))
        final = [(("dma", k), v) for k, v in self.dma_cnt.items()]
        block = stack.enter_context(nc.Block())
        ops = self.ops

        def run_engine(eng_name, eng):
            waited = {}
            if wait_phase is not None:
                eng.wait_ge(wait_phase[0], wait_phase[1])
            for o in ops:
                if o["eng"] != eng_name:
                    continue
                need = {}
                for d in o["deps"]:
                    t = ops[d]["ticket"]
                    if t is None:
                        continue
                    if eng_name == "pe" and ops[d]["eng"] == "pe":
                        continue
                    sk, v = t
                    if v > need.get(sk, 0):
                        need[sk] = v
                for sk, v in need.items():
                    if waited.get(sk, 0) >= v:
                        continue
                    eng.wait_ge(sems[sk], v)
                    waited[sk] = v
                ins = o["fn"](eng)
                if o["ticket"] is not None:
                    sk, v = o["ticket"]
                    ins.then_inc(sems[sk], o.get("inc", 16) if sk[0] == "dma" else 1)
            if eng_name == final_wait_eng:
                for sk, v in final:
                    eng.wait_ge(sems[sk], v)
            if signal_phase is not None:
                if eng_cnt[eng_name] > 0:
                    eng.wait_ge(sems[("eng", eng_name)], eng_cnt[eng_name])
                eng.sem_inc(signal_phase, 1)

        @block.sync
        def _(e):
            run_engine("sp", e)

        @block.tensor
        def _(e):
            run_engine("pe", e)

        @block.scalar
        def _(e):
            run_engine("act", e)

        @block.vector
        def _(e):
            run_engine("dve", e)

        @block.gpsimd
        def _(e):
            run_engine("pool", e)


def _rmsnorm_stats(P, src_tiles, src_keys, sq, ones, ps_stat, rstd, tag, dim):
    n = len(src_tiles)
    for k in range(n):
        sqk = sq[k % 2]
        P.op("act", lambda e, a=src_tiles[k], o=sqk: e.activation(out=o[:], in_=a, func=AF.Square),
             reads=[src_keys[k]], writes=[("sq", k % 2)])
        P.op("pe", lambda e, o=sqk, k=k: e.matmul(ps_stat[:], ones[:], o[:], start=(k == 0), stop=(k == n - 1)),
             reads=[("sq", k % 2), "ones"], writes=["ps_stat"])
    P.op("act", lambda e: e.activation(out=rstd[:], in_=ps_stat[:], func=AF.Ln, bias=EPS, scale=1.0 / dim),
         reads=["ps_stat"], writes=[tag])
    P.op("act", lambda e: e.activation(out=rstd[:], in_=rstd[:], func=AF.Exp, scale=-0.5),
         reads=[tag], writes=[tag])


def build_phase_b(debug=False, NT=TPC // TT, ctx=None):
    nc = ctx["nc"] if ctx else bass.Bass("TRN2", target_bir_lowering=False)
    xT = nc.dram_tensor("xTb", [D, TPC], F32, kind="ExternalInput").ap()
    if ctx:
        sel_d = nc.dram_tensor("sel", [128, NCORES], F32, kind="ExternalInput").ap()
        ag_out = ctx["ag_out"]
    else:
        hT = nc.dram_tensor("hT", [D, TPC], BF16, kind="ExternalInput").ap()
    gam = nc.dram_tensor("gamb", [128, 4 * KT], F32, kind="ExternalInput").ap()
    wg = nc.dram_tensor("wg", [32, 128, D], F32, kind="ExternalInput").ap()
    wbm = nc.dram_tensor("wbm", [16, 128, 1024], F32, kind="ExternalInput").ap()
    wba = nc.dram_tensor("wba", [16, 128, 1024], F32, kind="ExternalInput").ap()
    wo = nc.dram_tensor("wo", [16, 128, D], F32, kind="ExternalInput").ap()
    wu = nc.dram_tensor("wu", [64, 128, D], F32, kind="ExternalInput").ap()
    wd = nc.dram_tensor("wd", [16, 128, DFF], F32, kind="ExternalInput").ap()
    outT = nc.dram_tensor("outT", [D, TPC], F32, kind="ExternalOutput").ap()
    dbg = {}
    if debug:
        for nm, dt in [("xn", BF16), ("mg", BF16), ("x1", F32), ("hn", BF16), ("y", F32), ("rstd", F32), ("mixo", F32)]:
            dbg[nm] = nc.dram_tensor("dbg_" + nm, [128, KT, TT], dt, kind="ExternalOutput").ap()

    with contextlib.ExitStack() as st:
        def sb(name, shape, dt):
            return st.enter_context(nc.sbuf_tensor("b_" + name, shape, dt))

        def dump(P, nm, buf, keys):
            if debug:
                P.op("sp", lambda e: e.dma_start(out=dbg[nm], in_=buf), reads=keys, dma="dbg_" + nm)
        x0 = sb("x0", [128, KT, TT], F32)
        xn = sb("xn", [128, KT, TT], BF16)
        hb = sb("hb", [128, 2 * KT, TT], BF16)
        mix = sb("mix", [128, KT, TT], F32)
        sq = [sb("sq0", [128, TT], BF16), sb("sq1", [128, TT], BF16)]
        rstd = sb("rstd", [128, TT], F32)
        ones = sb("ones", [128, 128], BF16)
        gm = sb("gam_sb", [128, 4 * KT], F32)
        tmpa = [sb("tmpa%d" % i, [128, TT], F32) for i in range(4)]
        tmpb = [sb("tmpb%d" % i, [128, TT], F32) for i in range(2)]
        NW = 6
        wslab = [sb("wslab%d" % i, [128, 4096], BF16) for i in range(NW)]
        ps = [st.enter_context(nc.psum_tensor("ps%d" % i, [128, TT], F32)) for i in range(8)]
        ps_stat = ps[7]

        P = Prog(nc, tag="b")
        if ctx:
            cand = [sb("cand%d" % i, [128, NCORES, TT], BF16) for i in range(2)]
            selb = sb("selb", [128, NCORES], F32)
            P.op("pool", lambda e: e.collective_compute("AllGather", ALU.bypass, replica_groups=[list(range(NCORES))],
                                                        ins=[ctx["ag_in_t"].ap().opt()], outs=[ctx["ag_out_t"].ap().opt()]),
                 writes=["agout"], dma="cc", inc=1)
            P.op("sp", lambda e: e.dma_start(out=selb[:], in_=sel_d[:]), writes=["sel"], dma="g:c")
        P.op("pool", lambda e: e.memset(ones[:], 1.0), writes=["ones"])
        P.op("sp", lambda e: e.dma_start(out=gm[:], in_=gam[:]), writes=["gam"], dma="g:c")
        wcount = [0]

        def load_w(src_ap, ncols):
            i = wcount[0] % NW
            wcount[0] += 1
            P.op("pool", lambda e, i=i: e.dma_start(out=wslab[i][:, 0:ncols], in_=src_ap),
                 writes=[("w", i)], dma="w%d" % i)
            return i

        bank_rr = [0]

        def next_bank():
            b = bank_rr[0] % 7
            bank_rr[0] += 1
            return b

        xT_v = xT.rearrange("(k p) t -> p k t", p=128)
        if not ctx:
            hT_v = hT.rearrange("(k p) t -> p k t", p=128)
        outT_v = outT.rearrange("(k p) t -> p k t", p=128)

        for tt in range(NT):
            tsl = slice(tt * TT, (tt + 1) * TT)
            for k4 in range(4):
                P.op("sp", lambda e, k4=k4, tsl=tsl: e.dma_start(out=x0[:, 4 * k4:4 * k4 + 4, :], in_=xT_v[:, 4 * k4:4 * k4 + 4, tsl]),
                     writes=[("x0", k) for k in range(4 * k4, 4 * k4 + 4)], dma="x%d" % k4)
            if not ctx:
                for k4 in range(4):
                    P.op("sp", lambda e, k4=k4, tsl=tsl: e.dma_start(out=hb[:, 4 * k4:4 * k4 + 4, :], in_=hT_v[:, 4 * k4:4 * k4 + 4, tsl]),
                         writes=[("hb", k) for k in range(4 * k4, 4 * k4 + 4)], dma="h%d" % k4)
            else:
                for kt in range(KT):
                    r0 = (kt % 8) * 256 + (0 if kt < 8 else 128)
                    src = ag_out[r0:r0 + 128, :].rearrange("p (c t) -> p c t", c=NCORES)[:, :, tt * TT:(tt + 1) * TT]
                    cb = cand[kt % 2]
                    P.op("sp", lambda e, cb=cb, src=src: e.dma_start(out=cb[:], in_=src), reads=["agout"], writes=[("cand", kt % 2)], dma="h%d" % (kt % 2))
                    P.op("dve", lambda e, cb=cb, kt=kt: e.tensor_scalar(out=hb[:, kt, :], in0=cb[:, 0, :], scalar1=selb[:, 0:1], scalar2=None, op0=ALU.mult),
                         reads=[("cand", kt % 2), "sel"], writes=[("hb", kt)])
                    for cc in range(1, NCORES):
                        P.op("dve", lambda e, cb=cb, kt=kt, cc=cc: e.scalar_tensor_tensor(out=hb[:, kt, :], in0=cb[:, cc, :], scalar=selb[:, cc:cc + 1], in1=hb[:, kt, :],
                                                                                       op0=ALU.mult, op1=ALU.add),
                             reads=[("cand", kt % 2), "sel", ("hb", kt)], writes=[("hb", kt)])
            _rmsnorm_stats(P, [x0[:, k, :] for k in range(KT)], [("x0", k) for k in range(KT)], sq, ones, ps_stat, rstd, "rstd", D)
            for k in range(KT):
                P.op("dve", lambda e, k=k: e.scalar_tensor_tensor(out=xn[:, k, :], in0=x0[:, k, :], scalar=gm[:, k:k + 1], in1=rstd[:],
                                                                   op0=ALU.mult, op1=ALU.mult),
                     reads=[("x0", k), "gam", "rstd"], writes=[("xn", k)])
            if tt == 0:
                dump(P, "xn", xn[:], [("xn", k) for k in range(KT)])
                dump(P, "rstd", rstd[:], ["rstd"]) if False else None
            for j in range(KT):
                pb = [ps[(4 * (j % 2)) + i] for i in range(4)]
                pk = [("ps", (4 * (j % 2)) + i) for i in range(4)]
                if j % 2 == 1:
                    pk[3] = "ps_stat"
                srcs = [(wg[j], D, xn, 0, ("xn",)), (wg[16 + j], D, xn, 0, ("xn",)),
                        (wbm[j], 1024, hb, 0, ("hb",)), (wba[j], 1024, hb, 8, ("hb",))]
                for gi, (wsrc, K, act_t, koff, kk) in enumerate(srcs):
                    wi = load_w(wsrc, K)
                    nk = K // 128
                    for k in range(nk):
                        P.op("pe", lambda e, wi=wi, k=k, nk=nk, gi=gi, act_t=act_t, koff=koff, pb=pb:
                             e.matmul(pb[gi][:], wslab[wi][:, k * 128:(k + 1) * 128], act_t[:, koff + k, :], start=(k == 0), stop=(k == nk - 1)),
                             reads=[("w", wi), (kk[0], koff + k)], writes=[pk[gi]])
                P.op("act", lambda e, pb=pb: e.activation(out=tmpa[0][:], in_=pb[0][:], func=AF.Sigmoid), reads=[pk[0]], writes=[("ta", 0)])
                P.op("act", lambda e, pb=pb: e.activation(out=tmpa[1][:], in_=pb[1][:], func=AF.Sigmoid), reads=[pk[1]], writes=[("ta", 1)])
                P.op("dve", lambda e, pb=pb: e.tensor_tensor(out=tmpa[2][:], in0=tmpa[0][:], in1=pb[2][:], op=ALU.mult), reads=[("ta", 0), pk[2]], writes=[("ta", 2)])
                P.op("dve", lambda e, pb=pb: e.tensor_tensor(out=tmpa[3][:], in0=tmpa[1][:], in1=pb[3][:], op=ALU.mult), reads=[("ta", 1), pk[3]], writes=[("ta", 3)])
                P.op("dve", lambda e, j=j: e.tensor_tensor(out=hb[:, KT + j, :], in0=tmpa[2][:], in1=tmpa[3][:], op=ALU.add),
                     reads=[("ta", 2), ("ta", 3)], writes=[("hb", KT + j)])
            if tt == 0:
                dump(P, "mg", hb[:, KT:2 * KT, :], [("hb", KT + k) for k in range(KT)])
            for j in range(KT):
                b = next_bank()
                wi = load_w(wo[j], D)
                for k in range(KT):
                    P.op("pe", lambda e, wi=wi, k=k, b=b: e.matmul(ps[b][:], wslab[wi][:, k * 128:(k + 1) * 128], hb[:, KT + k, :], start=(k == 0), stop=(k == KT - 1)),
                         reads=[("w", wi), ("hb", KT + k)], writes=[("ps", b)])
                P.op("act", lambda e, j=j, b=b: e.activation(out=mix[:, j, :], in_=ps[b][:], func=AF.Copy), reads=[("ps", b)], writes=[("mix", j)])
            if tt == 0:
                dump(P, "mixo", mix[:], [("mix", k) for k in range(KT)])
            _rmsnorm_stats(P, [mix[:, k, :] for k in range(KT)], [("mix", k) for k in range(KT)], sq, ones, ps_stat, rstd, "rstd", D)
            for k in range(KT):
                P.op("dve", lambda e, k=k: e.scalar_tensor_tensor(out=mix[:, k, :], in0=mix[:, k, :], scalar=gm[:, KT + k:KT + k + 1], in1=rstd[:],
                                                                   op0=ALU.mult, op1=ALU.mult),
                     reads=[("mix", k), "gam", "rstd"], writes=[("mix", k)])
                P.op("dve", lambda e, k=k: e.tensor_tensor(out=x0[:, k, :], in0=x0[:, k, :], in1=mix[:, k, :], op=ALU.add),
                     reads=[("mix", k), ("x0", k)], writes=[("x0", k)])
            if tt == 0:
                dump(P, "x1", x0[:], [("x0", k) for k in range(KT)])
            _rmsnorm_stats(P, [x0[:, k, :] for k in range(KT)], [("x0", k) for k in range(KT)], sq, ones, ps_stat, rstd, "rstd", D)
            for k in range(KT):
                P.op("dve", lambda e, k=k: e.scalar_tensor_tensor(out=xn[:, k, :], in0=x0[:, k, :], scalar=gm[:, 2 * KT + k:2 * KT + k + 1], in1=rstd[:],
                                                                   op0=ALU.mult, op1=ALU.mult),
                     reads=[("x0", k), "gam", "rstd"], writes=[("xn", k)])
            if tt == 0:
                dump(P, "hn", xn[:], [("xn", k) for k in range(KT)])
            for half in range(2):
                for jj in range(32):
                    j = half * 32 + jj
                    b = next_bank()
                    wi = load_w(wu[j], D)
                    for k in range(KT):
                        P.op("pe", lambda e, wi=wi, k=k, b=b: e.matmul(ps[b][:], wslab[wi][:, k * 128:(k + 1) * 128], xn[:, k, :], start=(k == 0), stop=(k == KT - 1)),
                             reads=[("w", wi), ("xn", k)], writes=[("ps", b)])
                    tb = tmpb[jj % 2]
                    P.op("act", lambda e, b=b, tb=tb: e.activation(out=tb[:], in_=ps[b][:], func=AF.Relu), reads=[("ps", b)], writes=[("tb", jj % 2)])
                    P.op("dve", lambda e, jj=jj, tb=tb: e.tensor_tensor(out=hb[:, jj, :], in0=tb[:], in1=tb[:], op=ALU.mult),
                         reads=[("tb", jj % 2)], writes=[("hb", jj)])
                for j in range(KT):
                    b = next_bank()
                    wi = load_w(wd[j][:, half * 4096:(half + 1) * 4096], 4096)
                    for k in range(32):
                        P.op("pe", lambda e, wi=wi, k=k, b=b: e.matmul(ps[b][:], wslab[wi][:, k * 128:(k + 1) * 128], hb[:, k, :], start=(k == 0), stop=(k == 31)),
                             reads=[("w", wi), ("hb", k)], writes=[("ps", b)])
                    if half == 0:
                        P.op("act", lambda e, j=j, b=b: e.activation(out=mix[:, j, :], in_=ps[b][:], func=AF.Copy), reads=[("ps", b)], writes=[("mix", j)])
                    else:
                        P.op("dve", lambda e, j=j, b=b: e.tensor_tensor(out=mix[:, j, :], in0=mix[:, j, :], in1=ps[b][:], op=ALU.add),
                             reads=[("ps", b), ("mix", j)], writes=[("mix", j)])
            if tt == 0:
                dump(P, "y", mix[:], [("mix", k) for k in range(KT)])
            _rmsnorm_stats(P, [mix[:, k, :] for k in range(KT)], [("mix", k) for k in range(KT)], sq, ones, ps_stat, rstd, "rstd", D)
            for k in range(KT):
                P.op("dve", lambda e, k=k: e.scalar_tensor_tensor(out=mix[:, k, :], in0=mix[:, k, :], scalar=gm[:, 3 * KT + k:3 * KT + k + 1], in1=rstd[:],
                                                                   op0=ALU.mult, op1=ALU.mult),
                     reads=[("mix", k), "gam", "rstd"], writes=[("mix", k)])
                P.op("dve", lambda e, k=k: e.tensor_tensor(out=mix[:, k, :], in0=x0[:, k, :], in1=mix[:, k, :], op=ALU.add),
                     reads=[("mix", k), ("x0", k)], writes=[("mix", k)])
            for k4 in range(4):
                P.op("sp", lambda e, k4=k4, tsl=tsl: e.dma_start(out=outT_v[:, 4 * k4:4 * k4 + 4, tsl], in_=mix[:, 4 * k4:4 * k4 + 4, :]),
                     reads=[("mix", k) for k in range(4 * k4, 4 * k4 + 4)], dma="o%d" % k4)
        if ctx:
            P.emit(st, sem_stack=ctx["sem_stack"], wait_phase=(ctx["phase_sem"], 5))
        else:
            P.emit(st)
    return nc


def _slab(w, kt_major=True):
    K, N = w.shape
    a = w.reshape(K // 128, 128, N // 128, 128)
    return np.ascontiguousarray(a.transpose(2, 1, 0, 3)).reshape(N // 128, 128, K)


def _gam(v):
    return np.ascontiguousarray(v.reshape(KT, 128).T)


def phase_b_inputs(x, hT_all, w_in, w_branch_m, w_branch_a, w_out, w_up, w_down, gammas):
    wg = _slab(w_in[:, 7176:11272])
    wbm = _slab(w_branch_m)
    wba = _slab(w_branch_a)
    wo = _slab(w_out)
    wu = _slab(w_up)
    wd = _slab(w_down)
    gam = np.ascontiguousarray(np.concatenate([_gam(g) for g in gammas], axis=1))
    maps = []
    for c in range(NCORES):
        sl = slice(c * TPC, (c + 1) * TPC)
        maps.append(dict(xTb=np.ascontiguousarray(x[sl].T), gamb=gam,
                         wg=wg, wbm=wbm, wba=wba, wo=wo, wu=wu, wd=wd))
        if hT_all is not None:
            maps[-1]["hT"] = np.ascontiguousarray(hT_all[:, sl])
    return maps


def run_phase_b(x, hT_all, w_in, w_branch_m, w_branch_a, w_out, w_up, w_down, gammas):
    nc = build_phase_b()
    maps = phase_b_inputs(x, hT_all, w_in, w_branch_m, w_branch_a, w_out, w_up, w_down, gammas)
    res = run_bass_kernel_spmd(nc, maps, core_ids=list(range(NCORES)))
    out = np.concatenate([r["outT"].T for r in res.results], axis=0)
    return out


NTA = S // TT
NEG = -30000.0
TWO_PI_HI = 6.28125
TWO_PI_LO = 0.0019353071795864769
MAGIC = 12582912.0
PI_SAFE = 3.141592


def build_phase_a(ntiles=NTA, s_eff=S, max_ops=None, ctx=None):
    nc = ctx["nc"] if ctx else bass.Bass("TRN2", target_bir_lowering=False)
    xT = nc.dram_tensor("xT", [D, s_eff], F32, kind="ExternalInput").ap()
    pos = nc.dram_tensor("pos", [1, s_eff], I32, kind="ExternalInput").ap()
    gam = nc.dram_tensor("gam", [128, KT], F32, kind="ExternalInput").ap()
    wfm = nc.dram_tensor("wfm", [128, 6, D], F32, kind="ExternalInput").ap()
    wtm = nc.dram_tensor("wtm", [128, KT, 514], F32, kind="ExternalInput").ap()
    cwd = nc.dram_tensor("cw", [128, 20], F32, kind="ExternalInput").ap()
    misc = nc.dram_tensor("misc", [128, 4], F32, kind="ExternalInput").ap()
    normbc_d = nc.dram_tensor("normbc", [128, 128], F32, kind="ExternalInput").ap()
    tri_d = nc.dram_tensor("tri", [128, 128], F32, kind="ExternalInput").ap()
    identf_d = nc.dram_tensor("identf", [128, 128], F32, kind="ExternalInput").ap()
    mm_d = nc.dram_tensor("mm", [128, 896], F32, kind="ExternalInput").ap()
    e32_d = nc.dram_tensor("e32", [16, 16, 128], F32, kind="ExternalInput").ap()
    if ctx:
        hmT = ctx["ag_in"][0:128, :]
        haT = ctx["ag_in"][128:256, :]
    else:
        hmT = nc.dram_tensor("hmT", [128, s_eff], BF16, kind="ExternalOutput").ap()
        haT = nc.dram_tensor("haT", [128, s_eff], BF16, kind="ExternalOutput").ap()

    with contextlib.ExitStack() as st:
        def sb(name, shape, dt):
            return st.enter_context(nc.sbuf_tensor(name, shape, dt))
        wfm_sb = sb("wfm_sb", [128, 6, D], BF16)
        wtm_sb = sb("wtm_sb", [128, KT, 514], BF16)
        KTa = sb("KTa", [128, S], BF16)
        Va = sb("Va", [128, S // 128, 128], BF16)
        kmT = sb("kmT", [128, 256], F32)
        e32 = sb("e32_sb", [16, 16, 128], BF16)
        mm = sb("mm_sb", [128, 896], BF16)
        identb = sb("identb", [128, 128], BF16)
        identf = sb("identf_sb", [128, 128], F32)
        tri = sb("tri_sb", [128, 128], F32)
        onesf = sb("onesf", [128, 128], F32)
        onesb = sb("onesb", [128, 128], BF16)
        gm = sb("gm", [128, KT], F32)
        cw = sb("cw_sb", [128, 20], F32)
        msc = sb("msc", [128, 4], F32)
        nfb = sb("nfb", [128, 1], F32)
        normbc = sb("normbc_sb", [128, 128], F32)
        Cf = sb("Cf", [128, 2, 257], F32)
        Cb = sb("Cb", [128, 2, 257], BF16)
        xs = sb("xs", [128, KT, 256], F32)
        xn = sb("xn", [128, KT, TT], BF16)
        sq = [sb("sq%d" % i, [128, 256], BF16) for i in range(2)]
        rstd = sb("rstd", [128, TT], F32)
        cin = [sb("cin%d" % j, [128, TT + 3], F32) for j in range(4)]
        cacc = [sb("cacc0", [128, TT], F32)] * 2
        csg = [sb("csg0", [128, TT], F32)] * 2
        qk = [sb("qk%d" % j, [128, TT], BF16) for j in range(4)]
        posi = sb("posi", [128, TT], I32)
        ang = sb("ang", [128, TT], F32)
        rn1 = sb("rn1", [128, TT], F32)
        rr_ = sb("rr_", [128, TT], F32)
        sin_t = sb("sin_t", [128, TT], F32)
        cos_t = sb("cos_t", [128, TT], F32)
        pqf = sb("pqf", [128, TT], F32)
        pqs = sb("pqs", [128, TT], F32)
        qf = sb("qf", [128, TT], F32)
        qb = sb("qb", [128, TT], BF16)
        vb = [sb("vb%d" % i, [128, 257], BF16) for i in range(4)]
        ve = [sb("ve%d" % i, [128, 257], BF16) for i in range(4)]
        g2 = [sb("g2_%d" % i, [128, 128], F32) for i in range(4)]
        sg = sb("sg", [128, 128], F32)
        sml = sb("sml", [128, 336], F32)
        sD = [sb("sD%d" % i, [128, 128], BF16) for i in range(2)]
        ktok = [sb("ktok%d" % i, [128, 256], BF16) for i in range(2)]
        ho = sb("ho", [128, 128], BF16)
        hm_out = sb("hm_out", [128, TT], BF16)
        ha_out = sb("ha_out", [128, TT], BF16)
        g_sb = sb("g_sb", [128, 64], F32)
        top8 = sb("top8", [128, 8], F32)
        nbias = sb("nbias", [128, 64], BF16)
        biasT = sb("biasT", [16, 4, TT], BF16)
        PT = [sb("PT%d" % i, [128, TT], BF16) for i in range(3)]
        psb = [st.enter_context(nc.psum_tensor("psA%d" % i, [128, TT], F32)) for i in range(7)]
        pT = st.enter_context(nc.psum_tensor("psT", [128, 1024], BF16))

        P = Prog(nc, tag="a")

        def sc_(g, i):
            c = g * 32 + i * 8
            return sml[:, c:c + 1]

        def sg_(g, n=1):
            return sml[:, g * 32:(g + n) * 32:8]

        regc = {}

        def fill_reg(e, val):
            if val not in regc:
                regc[val] = e.to_reg(val)
            return regc[val]

        def se_(n):
            c = 224 + 8 * n
            return sml[:, c:c + 1]
        P.op("sp", lambda e: e.dma_start(out=gm[:], in_=gam[:]), writes=["gm"], dma="g:cs")
        P.op("sp", lambda e: e.dma_start(out=cw[:], in_=cwd[:]), writes=["cw"], dma="g:cs")
        P.op("sp", lambda e: e.dma_start(out=msc[:], in_=misc[:]), writes=["msc"], dma="g:cs")
        P.op("sp", lambda e: e.dma_start(out=normbc[:], in_=normbc_d[:]), writes=["normbc"], dma="g:cs")
        P.op("sp", lambda e: e.dma_start(out=tri[:], in_=tri_d[:]), writes=["tri"], dma="g:cs")
        P.op("sp", lambda e: e.dma_start(out=identf[:], in_=identf_d[:]), writes=["identf"], dma="g:cs")
        P.op("pool", lambda e: e.dma_start(out=identb[:], in_=identf_d[:]), writes=["identb"], dma="g:cp")
        P.op("pool", lambda e: e.dma_start(out=mm[:], in_=mm_d[:]), writes=["mm"], dma="g:cp")
        P.op("pool", lambda e: e.dma_start(out=e32[:], in_=e32_d[:]), writes=["e32"], dma="g:cp")
        for j in range(6):
            P.op("pool", lambda e, j=j: e.dma_start(out=wfm_sb[:, j, :], in_=wfm[:, j, :]), writes=[("wfm", j)], dma="g:cp")
        for k4 in range(4):
            P.op("pool", lambda e, k4=k4: e.dma_start(out=wtm_sb[:, 4 * k4:4 * k4 + 4, :], in_=wtm[:, 4 * k4:4 * k4 + 4, :]),
                 writes=[("wtm", k) for k in range(4 * k4, 4 * k4 + 4)], dma="g:c")
        P.op("dve", lambda e: e.memset(onesf[:], 1.0), writes=["onesf"])
        P.op("dve", lambda e: e.memset(onesb[:], 1.0), writes=["onesb"])
        P.op("dve", lambda e: e.memset(Cf[:], 0.0), writes=["Cf"])
        P.op("dve", lambda e: e.memset(Cb[:], 0.0), writes=["Cb"])
        P.op("dve", lambda e: e.memset(kmT[:], 0.0), writes=["kmT"])
        P.op("dve", lambda e: e.memset(g_sb[:], NEG), writes=["g_sb"])
        for j in range(4):
            P.op("dve", lambda e, j=j: e.memset(cin[j][:], 0.0), writes=[("cin", j)])
            P.op("dve", lambda e, j=j: e.memset(vb[j][:, 256:257], 1.0), writes=[("vb", j)])
        P.op("dve", lambda e: e.tensor_scalar(out=nfb[:], in0=msc[:, 1:2], scalar1=-1.0, scalar2=None, op0=ALU.mult), reads=["msc"], writes=["nfb"])

        xT_v = xT.rearrange("(k p) t -> p k t", p=128)
        pos_b = pos.partition_broadcast(128)
        rrb = [0]

        def nb4():
            b = rrb[0] % 4
            rrb[0] += 1
            return b

        for T in range(ntiles):
            t0 = T * TT
            P.op("sp", lambda e, t0=t0: e.dma_start(out=posi[:], in_=pos_b[:, 0, t0:t0 + TT]), writes=["posi"], dma="pos")
            for hf in range(2):
                h0 = t0 + hf * 256
                for k4 in range(4):
                    P.op("sp", lambda e, k4=k4, h0=h0: e.dma_start(out=xs[:, 4 * k4:4 * k4 + 4, :], in_=xT_v[:, 4 * k4:4 * k4 + 4, h0:h0 + 256]),
                         writes=[("xs", k) for k in range(4 * k4, 4 * k4 + 4)], dma="x%d" % k4)
                stat = psb[6]
                for k in range(KT):
                    P.op("act", lambda e, k=k: e.activation(out=sq[k % 2][:], in_=xs[:, k, :], func=AF.Square),
                         reads=[("xs", k)], writes=[("sq", k % 2)])
                    P.op("pe", lambda e, k=k, hf=hf: e.matmul(stat[:, hf * 256:(hf + 1) * 256], onesb[:], sq[k % 2][:], start=(k == 0), stop=(k == KT - 1)),
                         reads=[("sq", k % 2), "onesb"], writes=[("ps", 6)])
                hs = slice(hf * 256, (hf + 1) * 256)
                P.op("act", lambda e, hs=hs: e.activation(out=rstd[:, hs], in_=stat[:, hs], func=AF.Ln, bias=EPS, scale=1.0 / D),
                     reads=[("ps", 6)], writes=["rstd"])
                P.op("act", lambda e, hs=hs: e.activation(out=rstd[:, hs], in_=rstd[:, hs], func=AF.Exp, scale=-0.5),
                     reads=["rstd"], writes=["rstd"])
                for k in range(KT):
                    P.op("dve", lambda e, k=k, hs=hs: e.scalar_tensor_tensor(out=xn[:, k, hs], in0=xs[:, k, :], scalar=gm[:, k:k + 1], in1=rstd[:, hs],
                                                                            op0=ALU.mult, op1=ALU.mult),
                         reads=[("xs", k), "gm", "rstd"], writes=[("xn", k)])
            P.op("dve", lambda e: e.tensor_copy(out=ang[:], in_=posi[:]), reads=["posi"], writes=["ang"])
            P.op("dve", lambda e: e.tensor_scalar(out=ang[:], in0=ang[:], scalar1=msc[:, 2:3], scalar2=None, op0=ALU.mult), reads=["ang", "msc"], writes=["ang"])
            P.op("dve", lambda e: e.tensor_scalar(out=rn1[:], in0=ang[:], scalar1=float(1.0 / (2 * np.pi)), scalar2=MAGIC, op0=ALU.mult, op1=ALU.add),
                 reads=["ang"], writes=["rn1"])
            P.op("dve", lambda e: e.tensor_scalar(out=rn1[:], in0=rn1[:], scalar1=-MAGIC, scalar2=None, op0=ALU.add), reads=["rn1"], writes=["rn1"])
            P.op("dve", lambda e: e.scalar_tensor_tensor(out=rr_[:], in0=rn1[:], scalar=-TWO_PI_HI, in1=ang[:], op0=ALU.mult, op1=ALU.add),
                 reads=["rn1", "ang"], writes=["rr_"])
            P.op("dve", lambda e: e.scalar_tensor_tensor(out=rr_[:], in0=rn1[:], scalar=-TWO_PI_LO, in1=rr_[:], op0=ALU.mult, op1=ALU.add),
                 reads=["rn1", "rr_"], writes=["rr_"])
            P.op("dve", lambda e: e.tensor_scalar(out=rr_[:], in0=rr_[:], scalar1=-PI_SAFE, scalar2=PI_SAFE, op0=ALU.max, op1=ALU.min),
                 reads=["rr_"], writes=["rr_"])
            P.op("act", lambda e: e.activation(out=sin_t[:], in_=rr_[:], func=AF.Sin), reads=["rr_"], writes=["sin_t"])
            P.op("dve", lambda e: e.scalar_tensor_tensor(out=rn1[:], in0=rr_[:], scalar=-1.0, in1=rr_[:], op0=ALU.mult, op1=ALU.max), reads=["rr_"], writes=["rn1"])
            P.op("act", lambda e: e.activation(out=cos_t[:], in_=rn1[:], func=AF.Sin, scale=-1.0, bias=float(np.pi / 2)), reads=["rn1"], writes=["cos_t"])
            for j in range(6):
                b = nb4()
                for k in range(KT):
                    P.op("pe", lambda e, j=j, k=k, b=b: e.matmul(psb[b][:], wfm_sb[:, j, k * 128:(k + 1) * 128], xn[:, k, :], start=(k == 0), stop=(k == KT - 1)),
                         reads=[("wfm", j), ("xn", k)], writes=[("ps", b)])
                if j < 4:
                    P.op("pool", lambda e, j=j: e.tensor_copy(out=cin[j][:, 0:3], in_=cin[j][:, TT:TT + 3]), reads=[("cin", j)], writes=[("cin", j)])
                    P.op("act", lambda e, j=j, b=b: e.activation(out=cin[j][:, 3:TT + 3], in_=psb[b][:], func=AF.Copy), reads=[("ps", b)], writes=[("cin", j)])
                    ca = cacc[j % 2]
                    cs = csg[j % 2]
                    P.op("dve", lambda e, j=j, ca=ca: e.tensor_scalar(out=ca[:], in0=cin[j][:, 3:TT + 3], scalar1=cw[:, 4 * j + 3:4 * j + 4], scalar2=cw[:, 16 + j:17 + j],
                                                                       op0=ALU.mult, op1=ALU.add),
                         reads=[("cin", j), "cw"], writes=[("cacc", 0)])
                    for tap in range(3):
                        P.op("dve", lambda e, j=j, ca=ca, tap=tap: e.scalar_tensor_tensor(out=ca[:], in0=cin[j][:, tap:tap + TT], scalar=cw[:, 4 * j + tap:4 * j + tap + 1],
                                                                                         in1=ca[:], op0=ALU.mult, op1=ALU.add),
                             reads=[("cin", j), "cw", ("cacc", 0)], writes=[("cacc", 0)])
                    P.op("act", lambda e, ca=ca, cs=cs: e.activation(out=cs[:], in_=ca[:], func=AF.Sigmoid), reads=[("cacc", 0)], writes=[("csg", 0)])
                    sc = 1.0 if j < 2 else 0.0625
                    P.op("dve", lambda e, j=j, ca=ca, cs=cs, sc=sc: e.scalar_tensor_tensor(out=qk[j][:], in0=ca[:], scalar=sc, in1=cs[:], op0=ALU.mult, op1=ALU.mult),
                         reads=[("cacc", 0), ("csg", 0)], writes=[("qk", j)])
                else:
                    P.op("act", lambda e, b=b: e.activation(out=pqf[:], in_=psb[b][:], func=AF.Copy), reads=[("ps", b)], writes=["pqf"])
                    P.op("act", lambda e, b=b: e.activation(out=pqs[0:64, :], in_=psb[b][64:128, :], func=AF.Copy), reads=[("ps", b)], writes=["pqs0"])
                    P.op("act", lambda e, b=b: e.activation(out=pqs[64:128, :], in_=psb[b][0:64, :], func=AF.Copy), reads=[("ps", b)], writes=["pqs1"])
                    P.op("dve", lambda e: e.tensor_tensor(out=pqf[:], in0=pqf[:], in1=cos_t[:], op=ALU.mult), reads=["pqf", "cos_t"], writes=["pqf"])
                    P.op("dve", lambda e: e.scalar_tensor_tensor(out=pqs[:], in0=pqs[:], scalar=msc[:, 3:4], in1=sin_t[:], op0=ALU.mult, op1=ALU.mult),
                         reads=["pqs0", "pqs1", "sin_t", "msc"], writes=["pqs0", "pqs1"])
                    if j == 4:
                        P.op("pool", lambda e: e.tensor_tensor(out=qf[:], in0=pqf[:], in1=pqs[:], op=ALU.add), reads=["pqf", "pqs0", "pqs1"], writes=["qf"])
                        P.op("pool", lambda e: e.tensor_copy(out=qb[:], in_=qf[:]), reads=["qf"], writes=["qb"])
                    else:
                        P.op("pool", lambda e: e.tensor_tensor(out=pqf[:], in0=pqf[:], in1=pqs[:], op=ALU.add), reads=["pqf", "pqs0", "pqs1"], writes=["pqf"])
                        P.op("pool", lambda e, t0=t0: e.tensor_copy(out=KTa[:, t0:t0 + TT], in_=pqf[:]), reads=["pqf"], writes=[("KT", T)])
                        P.op("dve", lambda e, T=T: e.tensor_reduce(out=kmT[:, 8 * T:8 * T + 8:4], in_=pqf[:].rearrange("p (b k) -> p b k", b=2), axis=AX.X, op=ALU.add),
                             reads=["pqf"], writes=["kmT"])
                        P.op("dve", lambda e, T=T: e.tensor_scalar(out=kmT[:, 8 * T:8 * T + 8:4], in0=kmT[:, 8 * T:8 * T + 8:4], scalar1=1.0 / 256, scalar2=None, op0=ALU.mult),
                             reads=["kmT"], writes=["kmT"])
            for i in range(4):
                ts_ = slice(i * 128, (i + 1) * 128)
                b1 = nb4()
                for k in range(KT):
                    P.op("pe", lambda e, k=k, b1=b1, ts_=ts_: e.matmul(psb[b1][:, 0:258], xn[:, k, ts_], wtm_sb[:, k, 0:258], start=(k == 0), stop=(k == KT - 1)),
                         reads=[("xn", k), ("wtm", k)], writes=[("ps", b1)])
                b2 = nb4()
                for k in range(KT):
                    P.op("pe", lambda e, k=k, b2=b2, ts_=ts_: e.matmul(psb[b2][:, 0:256], xn[:, k, ts_], wtm_sb[:, k, 258:514], start=(k == 0), stop=(k == KT - 1)),
                         reads=[("xn", k), ("wtm", k)], writes=[("ps", b2)])
                P.op("act", lambda e, i=i, b1=b1: e.activation(out=vb[i][:, 0:256], in_=psb[b1][:, 0:256], func=AF.Copy), reads=[("ps", b1)], writes=[("vb", i)])
                P.op("act", lambda e, i=i, b1=b1: e.activation(out=sml[:, 288 + 8 * i:288 + 8 * i + 2], in_=psb[b1][:, 256:258], func=AF.Copy),
                     reads=[("ps", b1)], writes=[("raw", i)])
                P.op("dve", lambda e, i=i: e.tensor_scalar(out=sc_(1, i), in0=sml[:, 288 + 8 * i:289 + 8 * i], scalar1=msc[:, 0:1], scalar2=None, op0=ALU.add),
                     reads=[("raw", i), "msc"], writes=[("li", i)])
                P.op("act", lambda e, i=i: e.activation(out=sc_(2, i), in_=sml[:, 289 + 8 * i:290 + 8 * i], func=AF.Exp, scale=-1.0, bias=nfb[:, 0:1]),
                     reads=[("raw", i), "nfb"], writes=[("tmp", i)])
                P.op("act", lambda e, i=i: e.activation(out=sc_(0, i), in_=sc_(2, i), func=AF.Ln, bias=1.0), reads=[("tmp", i)], writes=[("nl", i)])
                P.op("act", lambda e, b2=b2: e.activation(out=sg[:], in_=psb[b2][:, 0:128], func=AF.Sigmoid), reads=[("ps", b2)], writes=["sg"])
                P.op("act", lambda e, b2=b2, T=T, i=i: e.activation(out=Va[:, 4 * T + i, :], in_=psb[b2][:, 128:256], func=AF.Copy), reads=[("ps", b2)], writes=[("V", T)])
                P.op("pool", lambda e, i=i: e.tensor_tensor(out=g2[i][:], in0=sg[:], in1=normbc[:], op=ALU.mult), reads=["sg", "normbc"], writes=[("g2", i)])
            P.op("pe", lambda e: e.matmul(psb[6][:, 0:4], tri[:], sg_(0), start=True, stop=True), reads=[("nl", i) for i in range(4)] + ["tri"], writes=[("ps", 6)])
            P.op("pe", lambda e: e.matmul(psb[6][:, 4:8], onesf[:], sg_(0), start=True, stop=True), reads=[("nl", i) for i in range(4)] + ["onesf"], writes=[("ps", 6)])
            P.op("act", lambda e: e.activation(out=sml[:, 320:328], in_=psb[6][:, 0:8], func=AF.Copy), reads=[("ps", 6)], writes=["cums"])
            P.op("dve", lambda e: e.tensor_tensor(out=sg_(2), in0=sml[:, 320:324], in1=sg_(1), op=ALU.add),
                 reads=["cums"] + [("li", i) for i in range(4)], writes=[("tmp", i) for i in range(4)])
            P.op("act", lambda e: e.activation(out=sg_(3), in_=sg_(2), func=AF.Exp), reads=[("tmp", i) for i in range(4)], writes=["ew"])
            P.op("act", lambda e: e.activation(out=sg_(4, 2), in_=sml[:, 320:328], func=AF.Exp, scale=-1.0), reads=["cums"], writes=["ainA"])
            P.op("dve", lambda e: e.tensor_tensor(out=sg_(6), in0=sg_(3), in1=sg_(5), op=ALU.mult), reads=["ew", "ainA"], writes=["eloc"])
            for i in range(4):
                own = 2 * T + i // 2
                qs_ = slice(i * 128, (i + 1) * 128)
                if own > 0:
                    P.op("pe", lambda e, qs_=qs_: e.matmul(psb[6][:, 64:128], qf[:, qs_], kmT[:, 0:256:4], start=True, stop=True), reads=["qf", "kmT"], writes=[("ps", 6)])
                    P.op("act", lambda e, own=own: e.activation(out=g_sb[:, 0:own], in_=psb[6][:, 64:64 + own], func=AF.Copy), reads=[("ps", 6)], writes=["g_sb"])
                P.op("dve", lambda e: e.max(out=top8[:], in_=g_sb[:]), reads=["g_sb"], writes=["top8"])
                P.op("dve", lambda e: e.tensor_scalar(out=nbias[:], in0=g_sb[:], scalar1=top8[:, 2:3], scalar2=NEG, op0=ALU.is_lt, op1=ALU.mult),
                     reads=["g_sb", "top8"], writes=["nbias"])
                P.op("pool", lambda e, own=own: e.affine_select(out=nbias[:], in_=nbias[:], pattern=[[-1, 64]], compare_op=ALU.is_ge, fill=fill_reg(e, 0.0),
                                                                base=own - 1, channel_multiplier=0), reads=["nbias"], writes=["nbias"])
                if i < 2:
                    P.op("pool", lambda e, own=own: e.affine_select(out=nbias[:], in_=nbias[:], pattern=[[1, 64]], compare_op=ALU.not_equal, fill=fill_reg(e, NEG),
                                                                    base=-(own + 1), channel_multiplier=0), reads=["nbias"], writes=["nbias"])
                for hh in range(4):
                    P.op("pe", lambda e, hh=hh: e.transpose(pT[0:16, 256 + hh * 128:256 + (hh + 1) * 128], nbias[:, 16 * hh:16 * hh + 16], identb[:]),
                         reads=["nbias", "identb"], writes=["pT"])
                    P.op("act", lambda e, hh=hh, qs_=qs_: e.activation(out=biasT[:, hh, qs_], in_=pT[0:16, 256 + hh * 128:256 + (hh + 1) * 128], func=AF.Copy),
                         reads=["pT"], writes=["biasT"])
            listM = []
            OPM = lambda *a, **k: listM.append((a, k))
            for i in range(4):
                cs_ = slice(i * 128, (i + 1) * 128)
                OPM("dve", lambda e, i=i: e.tensor_scalar(out=ve[i][:], in0=vb[i][:], scalar1=sc_(6, i), scalar2=None, op0=ALU.mult),
                     reads=[("vb", i), "eloc"], writes=[("ve", i)])
                bS = 5
                for dt in range(2):
                    OPM("pe", lambda e, dt=dt, bS=bS, cs_=cs_: e.matmul(psb[bS][:, 0:128], qk[2 + dt][:, cs_], qk[dt][:, cs_], start=(dt == 0), stop=(dt == 1)),
                         reads=[("qk", 2 + dt), ("qk", dt)], writes=[("ps", bS)])
                OPM("dve", lambda e, i=i, bS=bS: e.scalar_tensor_tensor(out=sD[i % 2][:], in0=psb[bS][:, 0:128], scalar=sc_(3, i), in1=tri[:],
                                                                        op0=ALU.mult, op1=ALU.mult),
                     reads=[("ps", bS), "ew", "tri"], writes=[("sD", i % 2)])
                for dt in range(2):
                    OPM("pe", lambda e, dt=dt, cs_=cs_: e.transpose(pT[:, dt * 128:(dt + 1) * 128], qk[2 + dt][:, cs_], identb[:]),
                         reads=[("qk", 2 + dt), "identb"], writes=["pT"])
                OPM("act", lambda e, i=i: e.activation(out=ktok[i % 2][:], in_=pT[:, 0:256], func=AF.Copy), reads=["pT", "pT"], writes=[("ktok", i % 2)])
                bH = 5
                OPM("pe", lambda e, bH=bH, cs_=cs_: e.matmul(psb[bH][:, 0:257], qk[0][:, cs_], Cb[:, 0, :], start=True, stop=False),
                     reads=[("qk", 0), "Cb"], writes=[("ps", bH)])
                OPM("pe", lambda e, bH=bH, cs_=cs_: e.matmul(psb[bH][:, 0:257], qk[1][:, cs_], Cb[:, 1, :], start=False, stop=False),
                     reads=[("qk", 1), "Cb"], writes=[("ps", bH)])
                OPM("pe", lambda e, bH=bH, i=i: e.matmul(psb[bH][:, 0:257], sD[i % 2][:], vb[i][:], start=False, stop=True),
                     reads=[("sD", i % 2), ("vb", i)], writes=[("ps", bH)])
                for kd in range(2):
                    OPM("pe", lambda e, kd=kd, i=i: e.matmul(psb[6][:, 0:257], ktok[i % 2][:, kd * 128:(kd + 1) * 128], ve[i][:], start=True, stop=True),
                        reads=[("ktok", i % 2), ("ve", i)], writes=[("ps", 6)])
                    OPM("dve", lambda e, kd=kd, i=i: e.scalar_tensor_tensor(out=Cf[:, kd, :], in0=Cf[:, kd, :], scalar=sc_(5, i), in1=psb[6][:, 0:257],
                                                                            op0=ALU.mult, op1=ALU.add),
                        reads=["Cf", "ainA", ("ps", 6)], writes=["Cf"])
                OPM("pool", lambda e: e.tensor_copy(out=Cb[:], in_=Cf[:]), reads=["Cf"], writes=["Cb"])
                c0 = 28
                OPM("dve", lambda e, bH=bH, i=i: e.tensor_tensor(out=se_(0), in0=psb[bH][:, 256:257], in1=sc_(4, i), op=ALU.mult),
                     reads=[("ps", bH), "ainA"], writes=["ep0"])
                OPM("dve", lambda e: e.scalar_tensor_tensor(out=se_(1), in0=se_(0), scalar=-1.0, in1=se_(0), op0=ALU.mult, op1=ALU.max), reads=["ep0"], writes=["ep1"])
                OPM("dve", lambda e: e.tensor_single_scalar(out=se_(1), in_=se_(1), scalar=1.0, op=ALU.max), reads=["ep1"], writes=["ep1"])
                OPM("dve", lambda e: e.reciprocal(out=se_(2), in_=se_(1)), reads=["ep1"], writes=["ep2"])
                OPM("dve", lambda e, i=i: e.tensor_tensor(out=se_(3), in0=se_(2), in1=sc_(4, i), op=ALU.mult),
                     reads=["ep2", "ainA"], writes=["ep3"])
                OPM("act", lambda e, bH=bH: e.activation(out=csg[0][:, 0:256], in_=psb[bH][:, 0:256], func=AF.Square, scale=se_(3), accum_out=se_(4)),
                     reads=[("ps", bH), "ep3"], writes=["ep4", ("csg", 0)])
                OPM("act", lambda e: e.activation(out=se_(5), in_=se_(4), func=AF.Ln, bias=EPS, scale=1.0 / 256), reads=["ep4"], writes=["ep5"])
                OPM("act", lambda e: e.activation(out=se_(5), in_=se_(5), func=AF.Exp, scale=-0.5), reads=["ep5"], writes=["ep5"])
                OPM("dve", lambda e: e.tensor_tensor(out=se_(6), in0=se_(3), in1=se_(5), op=ALU.mult), reads=["ep3", "ep5"], writes=["ep6"])
                OPM("dve", lambda e, bH=bH, i=i: e.scalar_tensor_tensor(out=ho[:], in0=psb[bH][:, 0:128], scalar=se_(6), in1=g2[i][:], op0=ALU.mult, op1=ALU.mult),
                     reads=[("ps", bH), "ep6", ("g2", i)], writes=["ho"])
                OPM("pe", lambda e: e.transpose(pT[:, 512:640], ho[:], identb[:]), reads=["ho", "identb"], writes=["pT"])
                OPM("act", lambda e, cs_=cs_: e.activation(out=hm_out[:, cs_], in_=pT[:, 512:640], func=AF.Copy), reads=["pT"], writes=["hm_out"])
            OPM("sp", lambda e, t0=t0: e.dma_start(out=hmT[:, t0:t0 + TT], in_=hm_out[:]), reads=["hm_out"], dma="om")
            listA = []
            OPA = lambda *a, **k: listA.append((a, k))
            nkt = 4 * T + 4
            pO, pSm = psb[3], psb[4]

            def QK(kk):
                b = kk % 3
                j = kk // 2
                cur = kk >= 4 * T
                OPA("pe", lambda e, kk=kk, b=b: e.matmul(psb[b][:], KTa[:, kk * 128:(kk + 1) * 128], qb[:], start=True, stop=False),
                     reads=[("KT", kk // 4), "qb"], writes=[("ps", b)])
                OPA("pe", lambda e, j=j, b=b, cur=cur: e.matmul(psb[b][:], e32[:, j % 16, :], biasT[:, j // 16, :], start=False, stop=(not cur)),
                     reads=["e32", "biasT"], writes=[("ps", b)])
                if cur:
                    m = kk - 4 * T
                    OPA("pe", lambda e, m=m, b=b: e.matmul(psb[b][:], identb[:], mm[:, (3 - m) * 128:(3 - m) * 128 + TT], start=False, stop=True),
                         reads=["identb", "mm"], writes=[("ps", b)])

            QK(0)
            if nkt > 1:
                QK(1)
            for kk in range(nkt):
                b = kk % 3
                OPA("act", lambda e, b=b: e.activation(out=PT[b][:], in_=psb[b][:], func=AF.Exp, scale=float(128 ** -0.5)), reads=[("ps", b)], writes=[("PT", b)])
                if kk + 2 < nkt:
                    QK(kk + 2)
                OPA("pe", lambda e, kk=kk, b=b, nkt=nkt: e.matmul(pO[:], Va[:, kk, :], PT[b][:], start=(kk == 0), stop=(kk == nkt - 1)),
                     reads=[("V", kk // 4), ("PT", b)], writes=[("ps", 3)])
                if kk == 0:
                    OPA("dve", lambda e, b=b: e.tensor_copy(out=cacc[0][:], in_=PT[b][:]), reads=[("PT", b)], writes=[("cacc", 0)])
                else:
                    OPA("dve", lambda e, b=b: e.tensor_tensor(out=cacc[0][:], in0=cacc[0][:], in1=PT[b][:], op=ALU.add),
                        reads=[("PT", b), ("cacc", 0)], writes=[("cacc", 0)])
            OPA("pe", lambda e: e.matmul(pSm[:], onesf[:], cacc[0][:], start=True, stop=True), reads=["onesf", ("cacc", 0)], writes=[("ps", 4)])
            OPA("dve", lambda e: e.reciprocal(out=pqs[:], in_=pSm[:]), reads=[("ps", 4)], writes=["pqs0", "pqs1"])
            OPA("dve", lambda e: e.tensor_tensor(out=ha_out[:], in0=pO[:], in1=pqs[:], op=ALU.mult), reads=[("ps", 3), "pqs0", "pqs1"], writes=["ha_out"])
            OPA("sp", lambda e, t0=t0: e.dma_start(out=haT[:, t0:t0 + TT], in_=ha_out[:]), reads=["ha_out"], dma="oa")
            nA, nM = len(listA), len(listM)
            ia = im = 0
            while ia < nA or im < nM:
                if im < nM and (ia >= nA or im * nA <= ia * nM):
                    a, k = listM[im]
                    im += 1
                else:
                    a, k = listA[ia]
                    ia += 1
                P.op(*a, **k)
        if max_ops is not None:
            P.ops = P.ops[:max_ops]
        if ctx:
            P.emit(st, sem_stack=ctx["sem_stack"], signal_phase=ctx["phase_sem"])
        else:
            P.emit(st)
    return nc


W_MQ, W_MK, W_MV, W_MO, W_MI, W_MF, W_AQ, W_AK, W_AV, W_GM = 0, 1024, 2048, 3072, 4096, 4100, 4104, 5128, 6152, 7176


def phase_a_inputs(x, positions, norm_mix_pre, w_in, conv_w, conv_b, i_bias, f_bias, mlstm_norm):
    xT = np.ascontiguousarray(x.T)
    gam = _gam(norm_mix_pre)
    tri = np.triu(np.ones((128, 128), np.float32))
    identf = np.eye(128, dtype=np.float32)
    mm = np.zeros((128, 896), np.float32)
    r = np.arange(128)[:, None]
    cq = np.arange(128)[None, :]
    mm[:, 256:384] = NEG
    mm[:, 384:512] = np.where(r > cq, NEG, 0.0)
    e32 = np.zeros((16, 16, 128), np.float32)
    for j in range(16):
        e32[j, j, :] = 1.0
    inv_freq = (1.0 / (10000.0 ** (np.arange(0, 128, 2, dtype=np.float32) / np.float32(128)))).astype(np.float32)
    maps = []
    for c in range(NCORES):
        h, half = c // 2, c % 2
        fm_cols = np.concatenate([np.arange(W_MQ + 256 * h, W_MQ + 256 * h + 256), np.arange(W_MK + 256 * h, W_MK + 256 * h + 256),
                                  np.arange(W_AQ + 128 * c, W_AQ + 128 * c + 128), np.arange(W_AK + 128 * c, W_AK + 128 * c + 128)])
        own = np.arange(W_MV + 256 * h + 128 * half, W_MV + 256 * h + 128 * half + 128)
        oth = np.arange(W_MV + 256 * h + 128 * (1 - half), W_MV + 256 * h + 128 * (1 - half) + 128)
        tm_cols = np.concatenate([own, oth, [W_MI + h, W_MF + h],
                                  np.arange(W_MO + 256 * h + 128 * half, W_MO + 256 * h + 128 * half + 128),
                                  np.arange(W_AV + 128 * c, W_AV + 128 * c + 128)])
        wf = w_in[:, fm_cols]
        wfm = np.ascontiguousarray(wf.reshape(KT, 128, 6, 128).transpose(1, 2, 0, 3)).reshape(128, 6, D)
        wt = w_in[:, tm_cols]
        wtm = np.ascontiguousarray(wt.reshape(KT, 128, 514).transpose(1, 0, 2))
        chs = [256 * h, 256 * h + 128, 1024 + 256 * h, 1024 + 256 * h + 128]
        cw = np.zeros((128, 20), np.float32)
        for j, ch in enumerate(chs):
            cw[:, 4 * j:4 * j + 4] = conv_w[:, ch:ch + 128].T
            cw[:, 16 + j] = conv_b[ch:ch + 128]
        misc = np.zeros((128, 4), np.float32)
        misc[:, 0] = i_bias[h]
        misc[:, 1] = f_bias[h]
        misc[:, 2] = np.concatenate([inv_freq, inv_freq])
        misc[0:64, 3] = -1.0
        misc[64:128, 3] = 1.0
        v0 = 256 * h + 128 * half
        normbc = np.ascontiguousarray(np.broadcast_to(mlstm_norm[v0:v0 + 128][None, :], (128, 128))).astype(np.float32)
        maps.append(dict(xT=xT, pos=np.ascontiguousarray(positions.reshape(1, S)).astype(np.int32), gam=gam, wfm=wfm, wtm=wtm, cw=cw, misc=misc,
                         normbc=normbc, tri=tri, identf=identf, mm=mm, e32=e32))
    return maps


def run_phase_a(x, positions, norm_mix_pre, w_in, conv_w, conv_b, i_bias, f_bias, mlstm_norm, ntiles=NTA):
    nc = build_phase_a(ntiles)
    maps = phase_a_inputs(x, positions, norm_mix_pre, w_in, conv_w, conv_b, i_bias, f_bias, mlstm_norm)
    res = run_bass_kernel_spmd(nc, maps, core_ids=list(range(NCORES)))
    hT_all = np.zeros((D, S), ml_dtypes.bfloat16)
    for c in range(NCORES):
        h, half = c // 2, c % 2
        v0 = 256 * h + 128 * half
        hT_all[v0:v0 + 128] = res.results[c]["hmT"]
        hT_all[1024 + 128 * c:1024 + 128 * c + 128] = res.results[c]["haT"]
    return hT_all


def build_fused():
    nc = bass.Bass("TRN2", target_bir_lowering=False)
    ag_in_t = nc.dram_tensor("ag_in", [256, S], BF16)
    ag_out_t = nc.dram_tensor("ag_out", [NCORES * 256, S], BF16)
    with contextlib.ExitStack() as sem_stack:
        phase_sem = sem_stack.enter_context(nc.semaphore("phase"))
        ctx = dict(nc=nc, ag_in=ag_in_t.ap(), ag_out=ag_out_t.ap(), ag_in_t=ag_in_t, ag_out_t=ag_out_t, sem_stack=sem_stack, phase_sem=phase_sem)
        build_phase_a(ctx=ctx)
        build_phase_b(ctx=ctx)
    return nc


def kernel_unfused(x, positions, norm_mix_pre, w_in, conv_w, conv_b, i_bias, f_bias, mlstm_norm,
                   w_branch_m, w_branch_a, w_out, norm_mix_post, norm_ffn_pre, w_up, w_down, norm_ffn_post):
    x2 = np.asarray(x)[0]
    hT_all = run_phase_a(x2, np.asarray(positions)[0], np.asarray(norm_mix_pre)[0], np.asarray(w_in)[0], np.asarray(conv_w)[0],
                         np.asarray(conv_b)[0], np.asarray(i_bias)[0], np.asarray(f_bias)[0], np.asarray(mlstm_norm)[0])
    gammas = [np.asarray(norm_mix_pre)[0], np.asarray(norm_mix_post)[0], np.asarray(norm_ffn_pre)[0], np.asarray(norm_ffn_post)[0]]
    out = run_phase_b(x2, hT_all, np.asarray(w_in)[0], np.asarray(w_branch_m)[0], np.asarray(w_branch_a)[0], np.asarray(w_out)[0],
                      np.asarray(w_up)[0], np.asarray(w_down)[0], gammas)
    return out[None].astype(np.float32)


def kernel(x, positions, norm_mix_pre, w_in, conv_w, conv_b, i_bias, f_bias, mlstm_norm,
           w_branch_m, w_branch_a, w_out, norm_mix_post, norm_ffn_pre, w_up, w_down, norm_ffn_post):
    x2 = np.asarray(x)[0]
    w_in2 = np.asarray(w_in)[0]
    gammas = [np.asarray(norm_mix_pre)[0], np.asarray(norm_mix_post)[0], np.asarray(norm_ffn_pre)[0], np.asarray(norm_ffn_post)[0]]
    maps_a = phase_a_inputs(x2, np.asarray(positions)[0], gammas[0], w_in2, np.asarray(conv_w)[0], np.asarray(conv_b)[0],
                            np.asarray(i_bias)[0], np.asarray(f_bias)[0], np.asarray(mlstm_norm)[0])
    maps_b = phase_b_inputs(x2, None, w_in2, np.asarray(w_branch_m)[0], np.asarray(w_branch_a)[0], np.asarray(w_out)[0],
                            np.asarray(w_up)[0], np.asarray(w_down)[0], gammas)
    maps = []
    for c in range(NCORES):
        m = dict(maps_a[c])
        m.update(maps_b[c])
        sel = np.zeros((128, NCORES), np.float32)
        sel[:, c] = 1.0
        m["sel"] = sel
        maps.append(m)
    nc = build_fused()
    res = run_bass_kernel_spmd(nc, maps, core_ids=list(range(NCORES)))
    out = np.concatenate([r["outT"].T for r in res.results], axis=0)
    return out[None].astype(np.float32)
```
